# Optimizing a Trainium2 kernel written in Bass

```python
import jax, jax.numpy as jnp
from jax import lax
import numpy as np

D_MODEL = 1024
BATCH = 8
SEQ = 2048
DEPTH = 4

GRID_W = 64
CTX_LEN = 256
HEAD_DIM = 64
MIX_HALF = D_MODEL // 2
N_Q_HEADS = MIX_HALF // HEAD_DIM
KV_GROUP = 4
N_KV_HEADS = N_Q_HEADS // KV_GROUP
WINDOW = 128
ATTN_BLOCK = 128
AX_DIM = HEAD_DIM // 2
ROPE_BASE = 10000.0
POOL_WIDTH = MIX_HALF
POOL_WINDOWS = (2, 4, 8, 16)
N_POOL_GROUPS = len(POOL_WINDOWS)
POOL_GROUP = POOL_WIDTH // N_POOL_GROUPS
CONV_WIDTH = 31
MLP_HIDDEN = 4 * D_MODEL
Q_WIDTH = N_Q_HEADS * HEAD_DIM
KV_WIDTH = N_KV_HEADS * HEAD_DIM
IN_WIDTH = POOL_WIDTH + Q_WIDTH + 2 * KV_WIDTH
N_EVEN = (DEPTH + 1) // 2
N_ODD = DEPTH // 2
NORM_EPS = 1e-6
NEG_LOGIT = -1e30

kernel_name = 'hybrid_pool_swa_conformer_dit'


def rms_norm(z, g):
    zf = z.astype(jnp.float32)
    zf = zf * lax.rsqrt(jnp.mean(zf * zf, axis=-1, keepdims=True) + NORM_EPS)
    return (zf * g.astype(jnp.float32)).astype(z.dtype)


def modulate(z, shift, scale):
    return z * (1 + scale) + shift


def rope_angles(S):
    rows = S // GRID_W
    row = jnp.broadcast_to(jnp.arange(rows, dtype=jnp.int32)[:, None], (rows, GRID_W)).reshape(S)
    col = jnp.broadcast_to(jnp.arange(GRID_W, dtype=jnp.int32)[None, :], (rows, GRID_W)).reshape(S)
    inv = ROPE_BASE ** (-jnp.arange(0, AX_DIM, 2, dtype=jnp.float32) / AX_DIM)
    return row.astype(jnp.float32)[:, None] * inv, col.astype(jnp.float32)[:, None] * inv


def rotate_axis(z, ang):
    cos = jnp.cos(ang)[:, None, :]
    sin = jnp.sin(ang)[:, None, :]
    z1, z2 = jnp.split(z, 2, axis=-1)
    return jnp.concatenate([z1 * cos - z2 * sin, z2 * cos + z1 * sin], axis=-1)


def axial_rope(z, ang_r, ang_c):
    zf = z.astype(jnp.float32)
    out = jnp.concatenate([rotate_axis(zf[..., :AX_DIM], ang_r), rotate_axis(zf[..., AX_DIM:], ang_c)], axis=-1)
    return out.astype(z.dtype)


def multiscale_pool(u, pool_w, pool_scale):
    B, S, _ = u.shape
    ug = u.astype(jnp.float32).reshape(B, S, N_POOL_GROUPS, POOL_GROUP)
    cs = jnp.concatenate([jnp.zeros((B, 1, N_POOL_GROUPS, POOL_GROUP), jnp.float32),
                          jnp.cumsum(ug, axis=1)], axis=1)
    t = jnp.arange(S)
    means = []
    for g, w in enumerate(POOL_WINDOWS):
        lo = jnp.clip(t - w // 2, 0, S)
        hi = jnp.clip(t + w - w // 2, 0, S)
        cnt = (hi - lo).astype(jnp.float32)[None, :, None]
        csg = cs[:, :, g]
        means.append((jnp.take(csg, hi, axis=1) - jnp.take(csg, lo, axis=1)) / cnt)
    pooled = jnp.stack(means, axis=2)
    d = (pooled - ug).astype(u.dtype)
    y = jnp.einsum('bsgc,gcd->bsgd', d, pool_w).reshape(B, S, POOL_WIDTH)
    return y * pool_scale


def window_attention(q, k, v, kc, vc, sink):
    B, S = q.shape[0], q.shape[1]
    L = kc.shape[1]
    nb = S // ATTN_BLOCK
    scale = HEAD_DIM ** -0.5
    qb = q.reshape(B, nb, ATTN_BLOCK, N_KV_HEADS, KV_GROUP, HEAD_DIM)
    pad = ((0, 0), (ATTN_BLOCK, ATTN_BLOCK), (0, 0), (0, 0))
    kp = jnp.pad(k, pad).reshape(B, nb + 2, ATTN_BLOCK, N_KV_HEADS, HEAD_DIM)
    vp = jnp.pad(v, pad).reshape(B, nb + 2, ATTN_BLOCK, N_KV_HEADS, HEAD_DIM)
    kw = jnp.concatenate([kp[:, :-2], kp[:, 1:-1], kp[:, 2:]], axis=2)
    vw = jnp.concatenate([vp[:, :-2], vp[:, 1:-1], vp[:, 2:]], axis=2)
    s_band = jnp.einsum('bnqkgd,bnjkd->bnkgqj', qb, kw).astype(jnp.float32) * scale
    qpos = (jnp.arange(nb) * ATTN_BLOCK)[:, None, None] + jnp.arange(ATTN_BLOCK)[None, :, None]
    kpos = (jnp.arange(nb) * ATTN_BLOCK - ATTN_BLOCK)[:, None, None] + jnp.arange(3 * ATTN_BLOCK)[None, None, :]
    valid = (jnp.abs(kpos - qpos) <= WINDOW) & (kpos >= 0) & (kpos < S)
    s_band = jnp.where(valid[None, :, None, None], s_band, NEG_LOGIT)
    s_ctx = jnp.einsum('bnqkgd,blkd->bnkgql', qb, kc).astype(jnp.float32) * scale
    s_sink = jnp.broadcast_to(sink.astype(jnp.float32).reshape(1, 1, N_KV_HEADS, KV_GROUP, 1, 1),
                              s_ctx.shape[:-1] + (1,))
    p = jax.nn.softmax(jnp.concatenate([s_sink, s_ctx, s_band], axis=-1), axis=-1)
    p_ctx = p[..., 1:1 + L].astype(v.dtype)
    p_band = p[..., 1 + L:].astype(v.dtype)
    out = (jnp.einsum('bnkgql,blkd->bnqkgd', p_ctx, vc)
           + jnp.einsum('bnkgqj,bnjkd->bnqkgd', p_band, vw))
    return out.reshape(B, S, Q_WIDTH)


def context_attention(qc, kc, vc, sink):
    B, L = qc.shape[0], qc.shape[1]
    qg = qc.reshape(B, L, N_KV_HEADS, KV_GROUP, HEAD_DIM)
    s = jnp.einsum('blkgd,bmkd->bkglm', qg, kc).astype(jnp.float32) * (HEAD_DIM ** -0.5)
    s_sink = jnp.broadcast_to(sink.astype(jnp.float32).reshape(1, N_KV_HEADS, KV_GROUP, 1, 1), s.shape[:-1] + (1,))
    p = jax.nn.softmax(jnp.concatenate([s_sink, s], axis=-1), axis=-1)[..., 1:].astype(vc.dtype)
    return jnp.einsum('bkglm,bmkd->blkgd', p, vc).reshape(B, L, Q_WIDTH)


def even_mixer(hx, hc, w_in, pool_w, pool_scale, sink, w_out, ang_r, ang_c, ctx_out):
    B, S, _ = hx.shape
    L = hc.shape[1]
    px = hx @ w_in
    u = px[..., :POOL_WIDTH]
    q = px[..., POOL_WIDTH:POOL_WIDTH + Q_WIDTH].reshape(B, S, N_Q_HEADS, HEAD_DIM)
    k = px[..., POOL_WIDTH + Q_WIDTH:POOL_WIDTH + Q_WIDTH + KV_WIDTH].reshape(B, S, N_KV_HEADS, HEAD_DIM)
    v = px[..., POOL_WIDTH + Q_WIDTH + KV_WIDTH:].reshape(B, S, N_KV_HEADS, HEAD_DIM)
    q = axial_rope(q, ang_r, ang_c)
    k = axial_rope(k, ang_r, ang_c)
    pkv = hc @ w_in[:, POOL_WIDTH + Q_WIDTH:]
    kc = pkv[..., :KV_WIDTH].reshape(B, L, N_KV_HEADS, HEAD_DIM)
    vc = pkv[..., KV_WIDTH:].reshape(B, L, N_KV_HEADS, HEAD_DIM)
    y_pool = multiscale_pool(u, pool_w, pool_scale)
    y_attn = window_attention(q, k, v, kc, vc, sink)
    out_x = jnp.concatenate([y_pool, y_attn], axis=-1) @ w_out
    if not ctx_out:
        return out_x, None
    pc = hc @ w_in[:, :POOL_WIDTH + Q_WIDTH]
    uc = pc[..., :POOL_WIDTH]
    qc = pc[..., POOL_WIDTH:].reshape(B, L, N_Q_HEADS, HEAD_DIM)
    yc = jnp.concatenate([multiscale_pool(uc, pool_w, pool_scale), context_attention(qc, kc, vc, sink)], axis=-1)
    return out_x, yc @ w_out


def conformer_conv(h, w_pw1, w_dw, b_dw, ln_g, ln_b, w_pw2):
    a = h @ w_pw1
    a1, a2 = jnp.split(a, 2, axis=-1)
    g = a1 * jax.nn.sigmoid(a2)
    half = CONV_WIDTH // 2
    y = lax.conv_general_dilated(g, w_dw[:, None, :], window_strides=(1,), padding=[(half, half)],
                                 dimension_numbers=('NWC', 'WIO', 'NWC'),
                                 feature_group_count=D_MODEL) + b_dw
    yf = y.astype(jnp.float32)
    mu = jnp.mean(yf, axis=-1, keepdims=True)
    var = jnp.mean(jnp.square(yf - mu), axis=-1, keepdims=True)
    yf = (yf - mu) * lax.rsqrt(var + NORM_EPS) * ln_g.astype(jnp.float32) + ln_b.astype(jnp.float32)
    y = (yf * jax.nn.sigmoid(yf)).astype(h.dtype)
    return y @ w_pw2


def sq_relu_mlp(h, w1, w2):
    return jnp.square(jax.nn.relu(h @ w1)) @ w2


def setup_inputs(seed: int = 0) -> dict:
    key = jax.random.key(seed)
    ks = jax.random.split(key, 24)
    f32 = jnp.float32
    D = D_MODEL

    def nrm(k, shape, s):
        return jax.random.normal(k, shape, f32) * s

    return {
        'x': nrm(ks[0], (BATCH, SEQ, D), 1.0),
        'c': nrm(ks[1], (BATCH, D), 1.0),
        'ctx': nrm(ks[2], (BATCH, CTX_LEN, D), 1.0),
        'c_ctx': nrm(ks[3], (D,), 1.0),
        'w_mod': nrm(ks[4], (DEPTH, D, 6 * D), 0.5 * D ** -0.5),
        'b_mod': nrm(ks[5], (DEPTH, 6 * D), 0.02),
        'norm1_g': 1.0 + nrm(ks[6], (DEPTH, D), 0.02),
        'norm2_g': 1.0 + nrm(ks[7], (DEPTH, D), 0.02),
        'mix_w_in': nrm(ks[8], (N_EVEN, D, IN_WIDTH), D ** -0.5),
        'pool_w': nrm(ks[9], (N_EVEN, N_POOL_GROUPS, POOL_GROUP, POOL_GROUP), POOL_GROUP ** -0.5),
        'pool_scale': 1.0 + nrm(ks[10], (N_EVEN, POOL_WIDTH), 0.02),
        'attn_sink': nrm(ks[11], (N_EVEN, N_Q_HEADS), 0.5),
        'mix_w_out': nrm(ks[12], (N_EVEN, POOL_WIDTH + Q_WIDTH, D), (POOL_WIDTH + Q_WIDTH) ** -0.5),
        'conv_w_pw1': nrm(ks[13], (N_ODD, D, 2 * D), D ** -0.5),
        'conv_w_dw': nrm(ks[14], (N_ODD, CONV_WIDTH, D), CONV_WIDTH ** -0.5),
        'conv_b_dw': nrm(ks[15], (N_ODD, D), 0.02),
        'conv_ln_g': 1.0 + nrm(ks[16], (N_ODD, D), 0.02),
        'conv_ln_b': nrm(ks[17], (N_ODD, D), 0.02),
        'conv_w_pw2': nrm(ks[18], (N_ODD, D, D), D ** -0.5),
        'mlp_w1': nrm(ks[19], (DEPTH, D, MLP_HIDDEN), D ** -0.5),
        'mlp_w2': nrm(ks[20], (DEPTH, MLP_HIDDEN, D), MLP_HIDDEN ** -0.5),
        'final_g': 1.0 + nrm(ks[21], (D,), 0.02),
    }


def reference(x, c, ctx, c_ctx, w_mod, b_mod, norm1_g, norm2_g, mix_w_in, pool_w, pool_scale,
              attn_sink, mix_w_out, conv_w_pw1, conv_w_dw, conv_b_dw, conv_ln_g, conv_ln_b,
              conv_w_pw2, mlp_w1, mlp_w2, final_g):
    S = x.shape[1]
    ang_r, ang_c = rope_angles(S)
    silu_c = jax.nn.silu(c)
    silu_cc = jax.nn.silu(c_ctx)
    last_reader = ((DEPTH - 1) // 2) * 2
    xc = ctx
    for l in range(DEPTH):
        mod = silu_c @ w_mod[l] + b_mod[l]
        sh1, sc1, g1, sh2, sc2, g2 = [m[:, None, :] for m in jnp.split(mod, 6, axis=-1)]
        need_ctx = l <= last_reader
        upd_ctx = l < last_reader
        if need_ctx:
            sh1c, sc1c, g1c, sh2c, sc2c, g2c = jnp.split(silu_cc @ w_mod[l] + b_mod[l], 6, axis=-1)
        hx = modulate(rms_norm(x, norm1_g[l]), sh1, sc1)
        if l % 2 == 0:
            i = l // 2
            hc = modulate(rms_norm(xc, norm1_g[l]), sh1c, sc1c)
            out_x, out_c = even_mixer(hx, hc, mix_w_in[i], pool_w[i], pool_scale[i], attn_sink[i],
                                      mix_w_out[i], ang_r, ang_c, upd_ctx)
        else:
            i = l // 2
            conv_args = (conv_w_pw1[i], conv_w_dw[i], conv_b_dw[i], conv_ln_g[i], conv_ln_b[i], conv_w_pw2[i])
            out_x = conformer_conv(hx, *conv_args)
            if upd_ctx:
                hc = modulate(rms_norm(xc, norm1_g[l]), sh1c, sc1c)
                out_c = conformer_conv(hc, *conv_args)
        x = x + g1 * out_x
        x = x + g2 * sq_relu_mlp(modulate(rms_norm(x, norm2_g[l]), sh2, sc2), mlp_w1[l], mlp_w2[l])
        if upd_ctx:
            xc = xc + g1c * out_c
            xc = xc + g2c * sq_relu_mlp(modulate(rms_norm(xc, norm2_g[l]), sh2c, sc2c), mlp_w1[l], mlp_w2[l])
    return rms_norm(x, final_g)
```

```python
import numpy as np
from contextlib import ExitStack
import concourse.bass as bass
import concourse.mybir as mybir
from concourse.bass_utils import run_bass_kernel_spmd

F32 = mybir.dt.float32
BF16 = mybir.dt.bfloat16
ALU = mybir.AluOpType
AF = mybir.ActivationFunctionType

D = 1024
S = 2048
L = 256
NT = S + L
DEPTH = 4
EPS = 1e-6
NSLOT = 4
CONVW = 31
HALF = 15


class Eng:
    def __init__(self, name, sem):
        self.name = name
        self.sem = sem
        self.count = 0
        self.q = []
        self.seen = {}


class Sched:
    def __init__(self, nc, stack, dry=False):
        self.nc = nc
        self.stack = stack
        self.dry = dry
        self.E = {}
        for n in ("pe", "act", "dve", "pool", "sp"):
            self.E[n] = Eng(n, None if dry else stack.enter_context(nc.semaphore("s_" + n)))
        self.lastw = {}
        self.lastr = {}
        self.nchan = 0

    def chan(self, name):
        return Eng(name, None if self.dry else self.stack.enter_context(self.nc.semaphore("c_" + name)))

    def _deps(self, eng, reads, writes):
        deps = {}

        def add(e, c, kind):
            if e is eng:
                if eng.name == "pe":
                    return
            if e.name not in deps or deps[e.name][1] < c:
                deps[e.name] = (e, c)

        for key in reads:
            w = self.lastw.get(key)
            if w:
                add(w[0], w[1], "raw")
        for key in writes:
            w = self.lastw.get(key)
            if w:
                add(w[0], w[1], "waw")
            for (e, c) in self.lastr.get(key, {}).values():
                add(e, c, "war")
        return deps

    def _emit_waits(self, eng, deps):
        for (e, c) in deps.values():
            if eng.seen.get(e.name, 0) < c:
                eng.seen[e.name] = c
                eng.q.append(("wait", e.sem, c))

    def _record(self, eng, cnt, reads, writes):
        for key in reads:
            self.lastr.setdefault(key, {})[eng.name] = (eng, cnt)
        for key in writes:
            self.lastw[key] = (eng, cnt)
            self.lastr[key] = {}

    def op(self, engname, meth, args, kw=None, reads=(), writes=()):
        if self.dry:
            return
        eng = self.E[engname]
        self._emit_waits(eng, self._deps(eng, reads, writes))
        eng.count += 1
        eng.q.append(("op", meth, args, kw or {}, eng.sem, 1))
        self._record(eng, eng.count, reads, writes)

    def mm(self, out, lhsT, rhs, start, stop, reads=(), writes=(), last=True):
        if self.dry:
            return
        eng = self.E["pe"]
        self._emit_waits(eng, self._deps(eng, reads, writes))
        cnt = eng.count + 1
        kw = {"start": bool(start), "stop": bool(stop)}
        if last:
            eng.count = cnt
            eng.q.append(("op", "matmul", (out, lhsT, rhs), kw, eng.sem, 1))
        else:
            eng.q.append(("op", "matmul", (out, lhsT, rhs), kw, None, 0))
        self._record(eng, cnt, reads, writes)

    def dma(self, qname, chan, out, in_, reads=(), writes=()):
        if self.dry:
            return
        q = self.E[qname]
        self._emit_waits(q, self._deps(chan, reads, writes))
        chan.count += 16
        q.q.append(("op", "dma_start", (), {"out": out, "in_": in_}, chan.sem, 16))
        self._record(chan, chan.count, reads, writes)

    def fence(self, qname, names=("pe", "act", "dve")):
        if self.dry:
            return
        q = self.E[qname]
        for b in names:
            eb = self.E[b]
            if q.seen.get(eb.name, 0) < eb.count:
                q.seen[eb.name] = eb.count
                q.q.append(("wait", eb.sem, eb.count))

    def barrier(self, names=("pe", "act", "dve"), waiters=("act", "dve")):
        if self.dry:
            return
        for a in waiters:
            ea = self.E[a]
            for b in names:
                if a == b:
                    continue
                eb = self.E[b]
                if ea.seen.get(eb.name, 0) < eb.count:
                    ea.seen[eb.name] = eb.count
                    ea.q.append(("wait", eb.sem, eb.count))

    def flush(self, block):
        def run(e, q):
            for it in q:
                if it[0] == "wait":
                    e.wait_ge(it[1], it[2])
                else:
                    ins = getattr(e, it[1])(*it[2], **it[3])
                    if it[4] is not None:
                        ins.then_inc(it[4], it[5])

        E = self.E

        @block.tensor
        def _(e):
            run(e, E["pe"].q)

        @block.scalar
        def _(e):
            run(e, E["act"].q)

        @block.vector
        def _(e):
            run(e, E["dve"].q)

        @block.gpsimd
        def _(e):
            run(e, E["pool"].q)

        @block.sync
        def _(e):
            run(e, E["sp"].q)


class Arena:
    def __init__(self, t, nwords):
        self.t = t
        self.n = nwords
        self.off = 0

    def reset(self):
        self.off = 0

    def _take(self, nw):
        a = self.off
        self.off += nw
        assert self.off <= self.n, ("arena overflow", self.off, self.n)
        return self.t[:, a:a + nw]

    def f32(self, *free):
        n = int(np.prod(free))
        ap = self._take(n)
        if len(free) == 2:
            ap = ap.rearrange("p (a b) -> p a b", a=free[0])
        return ap

    def bf16(self, *free):
        n = int(np.prod(free))
        nw = (n + 1) // 2
        ap = self._take(nw).bitcast(BF16)[:, 0:n]
        if len(free) == 2:
            ap = ap.rearrange("p (a b) -> p a b", a=free[0])
        return ap


VEC_LAYOUT = [("n1g", 32), ("n2g", 32), ("bmod", 192), ("pscale", 8), ("sinkc", 8), ("bdw", 16),
              ("lng", 16), ("lnb", 16), ("fing", 8), ("cvec", 16)]


def x_tiles(with_ctx):
    t = [(i * 512, 512, 0, i) for i in range(4)]
    if with_ctx:
        t.append((S, L, 1, 4))
    return t


def build(n_layers=DEPTH, final_norm=True, debug=False):
    nc = bass.Bass("TRN2", target_bir_lowering=False)
    dt_in = {}

    def din(name, shape):
        dt_in[name] = nc.dram_tensor(name, list(shape), F32, kind="ExternalInput").ap()
        return dt_in[name]

    xT = din("xT", [D, NT])
    din("cvec", [128, 16])
    w_mod = din("w_mod", [DEPTH, D, 6 * D])
    din("bmod", [128, 192])
    din("n1g", [128, 32])
    din("n2g", [128, 32])
    w_in = din("w_in", [2, D, 1280])
    pool_w = din("pool_w", [2, 4, 128, 128])
    din("pscale", [128, 8])
    din("sinkc", [128, 8])
    w_out = din("w_out", [2, D, D])
    pw1 = din("pw1", [2, D, 2 * D])
    din("dwT", [128, 496])
    din("bdw", [128, 16])
    din("lng", [128, 16])
    din("lnb", [128, 16])
    pw2 = din("pw2", [2, D, D])
    w1 = din("w1", [DEPTH, D, 4 * D])
    w2 = din("w2", [DEPTH, 4 * D, D])
    din("fing", [128, 8])
    identd = din("ident", [128, 128])
    bandsd = din("bands", [128, 20 * 128])
    masksd = din("masks", [128, 2 * 128])
    ropec = din("ropec", [128, NT])
    ropes = din("ropes", [128, NT])
    permd = din("permc", [128, 128])
    outT = nc.dram_tensor("outT", [D, S], F32, kind="ExternalOutput").ap()
    if debug:
        dbgf = nc.dram_tensor("dbgf", [128, 8 * NT + 8 * NT + 96 + 64], F32, kind="ExternalOutput").ap()
        dbgb = nc.dram_tensor("dbgb", [128, 8 * NT + NT + 18 * 128 + 8 * 512 + 4 * 512 + 6 * 512], BF16, kind="ExternalOutput").ap()

    with ExitStack() as st:
        def SB(name, shape, dt):
            return st.enter_context(nc.sbuf_tensor(name, shape, dt))

        X = SB("X", [128, 8, NT], F32)
        H = SB("H", [128, 8, NT], BF16)
        WR = [SB("wr%d" % i, [128, 4096], BF16) for i in range(NSLOT)]
        nvec = sum(n for _, n in VEC_LAYOUT)
        VEC = SB("VEC", [128, nvec], F32)
        MODV2 = [SB("MODV%d" % i, [128, 96], F32) for i in range(2)]
        AB2 = [SB("AB%d" % i, [128, 64], F32) for i in range(2)]
        RST = SB("RST", [128, 512], F32)
        TMP = SB("TMP", [128, 2, 512], F32)
        SC = SB("SC", [128, 16], BF16)
        ONES = SB("ONES", [128, 128], BF16)
        ONE1 = SB("ONE1", [128, 128], BF16)
        IDB = SB("IDB", [128, 128], BF16)
        EPST = SB("EPST", [128, 1], F32)
        ARENA_W = 14884
        ART = SB("ARENA", [128, ARENA_W], F32)
        PS = [st.enter_context(nc.psum_tensor("ps%d" % i, [128, 512], F32)) for i in range(8)]

        voff = {}
        o = 0
        for nm, n in VEC_LAYOUT:
            voff[nm] = o
            o += n

        def vec(nm, a, b):
            return VEC[:, voff[nm] + a: voff[nm] + b]

        def emit(Sd, plan):
            dry = Sd.dry
            slab_state = {"issued": 0, "consumed": 0}
            slot_ch = [Sd.chan("w%d" % i) for i in range(NSLOT)]
            ch_const = Sd.chan("const")
            ch_constp = Sd.chan("constp")
            ch_x = Sd.chan("x")
            ch_rope = [Sd.chan("rope%d" % i) for i in range(2)]
            ch_out = {}

            def out_chan(key):
                if key not in ch_out:
                    ch_out[key] = Sd.chan("out%d" % len(ch_out))
                return ch_out[key]
            A = Arena(ART, ARENA_W)
            psk = lambda i: ("ps", i)

            def ACT(out, in_, func, reads, writes, **kw):
                Sd.op("act", "activation", (out, in_, func), kw, reads, writes)

            def TT(out, a, b, op, reads, writes):
                Sd.op("dve", "tensor_tensor", (out, a, b, op), None, reads, writes)

            def STT(out, in0, scalar, in1, op0, op1, reads, writes):
                Sd.op("dve", "scalar_tensor_tensor", (out, in0, scalar, in1, op0, op1), None, reads, writes)

            def DVE(meth, args, reads, writes):
                Sd.op("dve", meth, args, None, reads, writes)

            MM = Sd.mm

            def setw(keys, ch):
                if not dry:
                    for k in keys:
                        Sd.lastw[k] = (ch, ch.count)

            def slot_view(slot, nk, ncol):
                v = WR[slot][:, :].rearrange("p (k n) -> p k n", k=nk)
                if ncol != v.shape[2]:
                    v = v[:, :, 0:ncol]
                return v

            def issue_slab(i):
                src, nk = plan[i]
                slot = i % NSLOT
                Sd.dma("pool", slot_ch[slot], slot_view(slot, nk, src.shape[2]), src, writes=[("W", slot)])

            def get_slab(src, nk):
                if dry:
                    plan.append((src, nk))
                    i = len(plan) - 1
                else:
                    i = slab_state["consumed"]
                    assert plan[i][1] == nk and tuple(plan[i][0].shape) == tuple(src.shape)
                    while slab_state["issued"] < min(len(plan), i + NSLOT - 1):
                        issue_slab(slab_state["issued"])
                        slab_state["issued"] += 1
                    slab_state["consumed"] += 1
                slot = i % NSLOT
                return slot_view(slot, nk, src.shape[2]), ("W", slot)

            def wsl(w2d, c0, c1):
                return w2d[:, c0:c1].rearrange("(k p) n -> p k n", p=128)

            VK = [("vec", nm) for nm, _ in VEC_LAYOUT]
            for nm, n in VEC_LAYOUT:
                Sd.dma("sp", ch_const, vec(nm, 0, n), dt_in[nm], writes=[("vec", nm)])
            Sd.dma("pool", ch_constp, IDB[:, :], identd, writes=[("c", "idb")])
            setw(VK, ch_const)
            DVE("memset", (ONES[:, :], 1.0 / 1024.0), [], [("c", "ones")])
            DVE("memset", (ONE1[:, :], 1.0), [], [("c", "one1")])
            DVE("memset", (EPST[:, :], EPS), [], [("c", "eps")])
            ACT(SC[:, :], vec("cvec", 0, 16), AF.Silu, VK, [("c", "sc")])
            for (t0, n, s, ti) in x_tiles(True):
                Sd.dma("sp", ch_x, X[:, :, t0:t0 + n], xT[:, t0:t0 + n].rearrange("(c p) t -> p c t", p=128), writes=[("X", ti)])
            setw([("X", ti) for ti in range(5)], ch_x)

            ch_m = [Sd.chan("m%d" % i) for i in range(4)]
            mstate = {"ring": None, "issued": set()}

            def mod_ring_alloc(fence, nslots=2):
                mstate["ring"] = [A.bf16(8, 512) for _ in range(nslots)]
                if fence:
                    Sd.fence("pool")

            def mod_issue(l, j):
                slot = j % len(mstate["ring"])
                Sd.dma("pool", ch_m[slot], mstate["ring"][slot], wsl(w_mod[l], j * 512, (j + 1) * 512), writes=[("MW", slot)])
                mstate["issued"].add((l, j))

            def mod_slab(l, j):
                nsl = len(mstate["ring"])
                for jj_ in range(j, min(12, j + nsl)):
                    if (l, jj_) not in mstate["issued"]:
                        mod_issue(l, jj_)
                sl = mstate["ring"][j % nsl]
                wk = ("MW", j % nsl)
                for jj in range(4):
                    cc = j * 4 + jj
                    for k in range(8):
                        MM(PS[7][:, cc * 2:cc * 2 + 2], sl[:, k, jj * 128:(jj + 1) * 128], SC[:, k * 2:k * 2 + 2], k == 0, k == 7,
                           reads=[wk, ("c", "sc")], writes=[psk(7)], last=(k == 7))

            cur = {"lb": 0}

            def mod_finish(l, part=None):
                lb = l % 2
                mv = MODV2[lb][:, :].rearrange("p (j s) -> p j s", s=2)
                ab = AB2[lb][:, :].rearrange("p (n a c s) -> p n a c s", n=2, a=2, c=8)
                bm = vec("bmod", l * 48, (l + 1) * 48)
                pv = PS[7][:, 0:96].rearrange("p (j s) -> p j s", s=2)
                j0, j1 = {None: (0, 48), 0: (0, 16), 1: (16, 48)}[part]
                for s in range(2):
                    TT(mv[:, j0:j1, s], pv[:, j0:j1, s], bm[:, j0:j1], ALU.add, [psk(7)] + VK, [("modv", lb)])
                for nrm in ([0, 1] if part is None else [part]):
                    g = vec("n1g" if nrm == 0 else "n2g", l * 8, (l + 1) * 8)
                    for s in range(2):
                        sh = mv[:, (3 * nrm) * 8:(3 * nrm + 1) * 8, s]
                        scv = mv[:, (3 * nrm + 1) * 8:(3 * nrm + 2) * 8, s]
                        STT(ab[:, nrm, 0, :, s], scv, 1.0, g, ALU.add, ALU.mult, [("modv", lb)] + VK, [("ab", lb)])
                        DVE("tensor_copy", (ab[:, nrm, 1, :, s], sh), [("modv", lb)], [("ab", lb)])

            def abv(lb, nrm, a, c, s):
                o_ = ((nrm * 2 + a) * 8 + c) * 2 + s
                return AB2[lb][:, o_:o_ + 1]

            def gate(which, c, s):
                o_ = ((3 * which + 2) * 8 + c) * 2 + s
                return MODV2[cur["lb"]][:, o_:o_ + 1]

            def mkey():
                return ("modv", cur["lb"])

            def norm_A(tile):
                (t0, n, s, ti) = tile
                ACT(H[:, :, t0:t0 + n], X[:, :, t0:t0 + n], AF.Square, [("X", ti)], [("H", ti)])

            def norm_B_stats(tile, pb, rst):
                (t0, n, s, ti) = tile
                for c in range(8):
                    MM(PS[pb][:, 0:n], ONES[:, :], H[:, c, t0:t0 + n], c == 0, c == 7,
                       reads=[("H", ti), ("c", "ones")], writes=[psk(pb)], last=(c == 7))
                rk = ("rst", id(rst))
                ACT(rst[:, 0:n], PS[pb][:, 0:n], AF.Ln, [psk(pb), ("c", "eps")], [rk], bias=EPST[:, 0:1], scale=1.0)
                ACT(rst[:, 0:n], rst[:, 0:n], AF.Exp, [rk], [rk], scale=-0.5)

            def norm_B_mod(lb, nrm, tile, rst, outb=None, stt_eng="dve"):
                (t0, n, s, ti) = tile
                rk = ("rst", id(rst))
                for c in range(8):
                    if outb is not None:
                        STT(outb[:, c, 0:n], X[:, c, t0:t0 + n], vec("fing", c, c + 1), rst[:, 0:n], ALU.mult, ALU.mult,
                            [("X", ti), rk] + VK, [("outb", id(outb))])
                    else:
                        tb = c % 2
                        if stt_eng == "dve":
                            STT(TMP[:, tb, 0:n], X[:, c, t0:t0 + n], abv(lb, nrm, 0, c, s), rst[:, 0:n], ALU.mult, ALU.mult,
                                [("X", ti), rk, ("ab", lb)], [("tmp", tb)])
                            ACT(H[:, c, t0:t0 + n], TMP[:, tb, 0:n], AF.Identity, [("tmp", tb), ("ab", lb)], [("H", ti)],
                                bias=abv(lb, nrm, 1, c, s), scale=1.0)
                        else:
                            Sd.op("pool", "tensor_tensor", (TMP[:, tb, 0:n], X[:, c, t0:t0 + n], rst[:, 0:n], ALU.mult), None,
                                  [("X", ti), rk], [("tmp", tb)])
                            ACT(H[:, c, t0:t0 + n], TMP[:, tb, 0:n], AF.Identity, [("tmp", tb), ("ab", lb)], [("H", ti)],
                                bias=abv(lb, nrm, 1, c, s), scale=abv(lb, nrm, 0, c, s))
                if outb is not None:
                    Sd.dma("sp", out_chan(id(outb)), outT[:, t0:t0 + n].rearrange("(c p) t -> p c t", p=128), outb[:, :, 0:n],
                           reads=[("outb", id(outb))])

            RSTv = RST[:, :]

            def norm_B(lb, nrm, tile, pb, outb=None, stt_eng="dve"):
                norm_B_stats(tile, pb, RSTv)
                norm_B_mod(lb, nrm, tile, RSTv, outb, stt_eng)

            deferred = []

            def flush_deferred():
                while deferred:
                    deferred.pop(0)()

            def norm_seq(lb, nrm, tiles, pb):
                for tile in tiles:
                    norm_A(tile)
                    norm_B(lb, nrm, tile, pb)

            def mlp_phase(l, tiles, next_l, next_tiles, is_final):
                A.reset()
                rb = [A.f32(512) for _ in range(4)]
                hid = [A.bf16(4, 512) for _ in range(2)]
                outbs = [A.f32(8, 512) for _ in range(2)] if is_final else None
                if next_l is not None:
                    mod_ring_alloc(True)
                ps2 = [4, 5, 6]
                st2 = {"i": 0}
                modj = {"j": 0}
                next_ti = set(t[3] for t in next_tiles)

                def stage1(w1s, w1k, it, t0, n, ti):
                    hb = it % 2
                    for hc in range(4):
                        for k in range(8):
                            MM(PS[hc][:, 0:n], w1s[:, k, hc * 128:(hc + 1) * 128], H[:, k, t0:t0 + n], k == 0, k == 7,
                               reads=[w1k, ("H", ti)], writes=[psk(hc)], last=(k == 7))
                        ACT(rb[hc][:, 0:n], PS[hc][:, 0:n], AF.Relu, [psk(hc)], [("rb", hc)])
                        TT(hid[hb][:, hc, 0:n], rb[hc][:, 0:n], rb[hc][:, 0:n], ALU.mult, [("rb", hc)], [("hid", hb)])

                def stage2(w2s, w2k, it, t0, n, s, ti):
                    hb = it % 2
                    for oc in range(8):
                        pb = ps2[st2["i"] % 3]
                        st2["i"] += 1
                        for kk in range(4):
                            MM(PS[pb][:, 0:n], w2s[:, kk, oc * 128:(oc + 1) * 128], hid[hb][:, kk, 0:n], kk == 0, kk == 3,
                               reads=[w2k, ("hid", hb)], writes=[psk(pb)], last=(kk == 3))
                        STT(X[:, oc, t0:t0 + n], PS[pb][:, 0:n], gate(1, oc, s), X[:, oc, t0:t0 + n], ALU.mult, ALU.add,
                            [psk(pb), mkey(), ("X", ti)], [("X", ti)])

                pend = []

                def after_tile(it):
                    tile = tiles[it]
                    if pend:
                        doB(pend.pop(0))
                    if tile[3] in next_ti:
                        norm_A(tile)
                        pend.append(it)

                def doB(it):
                    tile = tiles[it]
                    if is_final:
                        norm_B(0, 0, tile, 7, outb=outbs[it % 2])
                    else:
                        norm_B(next_l % 2, 0, tile, 7, stt_eng="pool")

                for e_ in range(8):
                    last = (e_ == 7)
                    w1s, w1k = get_slab(wsl(w1[l], e_ * 512, (e_ + 1) * 512), 8)
                    w2s, w2k = get_slab(w2[l][e_ * 512:(e_ + 1) * 512, :].rearrange("(k p) n -> p k n", p=128), 4)
                    for it, (t0, n, s, ti) in enumerate(tiles):
                        stage1(w1s, w1k, it, t0, n, ti)
                        if it >= 1:
                            (pt0, pn, ps_, pti) = tiles[it - 1]
                            stage2(w2s, w2k, it - 1, pt0, pn, ps_, pti)
                            if last:
                                after_tile(it - 1)
                            if it == 1 and e_ == 0:
                                flush_deferred()
                            if it == 2 and next_l is not None and e_ < 6:
                                for _ in range(2):
                                    mod_slab(next_l, modj["j"])
                                    modj["j"] += 1
                    (pt0, pn, ps_, pti) = tiles[-1]
                    stage2(w2s, w2k, len(tiles) - 1, pt0, pn, ps_, pti)
                    if last:
                        after_tile(len(tiles) - 1)
                        while len(pend) > 1:
                            doB(pend.pop(0))
                        if pend:
                            deferred.append((lambda it_: (lambda: doB(it_)))(pend.pop(0)))
                    if next_l is not None and e_ == 6:
                        assert modj["j"] == 12
                        mod_finish(next_l)

            def proj_residual(slabs, ysrc, ykey, n, t0, s, ti, banks, rot):
                for oc in range(8):
                    pb = banks[rot["i"] % len(banks)]
                    rot["i"] += 1
                    sl, sk = slabs[oc // 4]
                    for k in range(8):
                        MM(PS[pb][:, 0:n], sl[:, k, (oc % 4) * 128:(oc % 4 + 1) * 128], ysrc[:, k, 0:n], k == 0, k == 7,
                           reads=[sk, ((ykey, k) if isinstance(ykey, str) else ykey)], writes=[psk(pb)], last=(k == 7))
                    STT(X[:, oc, t0:t0 + n], PS[pb][:, 0:n], gate(0, oc, s), X[:, oc, t0:t0 + n], ALU.mult, ALU.add,
                        [psk(pb), mkey(), ("X", ti)], [("X", ti)])

            def conv_mixer(l, segs):
                i2 = l // 2
                Sd.barrier()
                A.reset()
                GW = 512 + 2 * HALF
                Gs = A.bf16(8, GW)
                Zs = A.bf16(8, 512)
                dg = [A.bf16(CONVW, 128) for _ in range(2)]
                ybf = [A.bf16(512) for _ in range(1)]
                ysq = [A.bf16(512) for _ in range(1)]
                yb = A.f32(8, 512)
                sig = [A.f32(512) for _ in range(1)]
                sigh = [A.f32(32) for _ in range(1)]
                mean_s = A.f32(512)
                var_s = A.f32(512)
                rot = {"i": 0}
                dwt = A.f32(8, CONVW)
                Sd.fence("sp")
                Sd.dma("sp", ch_const, dwt, dt_in["dwT"][:, i2 * 248:(i2 + 1) * 248].rearrange("p (c k) -> p c k", c=8), writes=[("dwt",)])
                setw([("dwt",)], ch_const)
                cnt = {"sg": 0, "dg": 0, "yb": 0}
                idb_b = IDB[:, :].unsqueeze(1).broadcast_to([128, CONVW, 128])
                def seg_pw1(seg, hook=None):
                    (t0, n, s, ti, lo, hi) = seg
                    hasL = (t0 - HALF) >= lo
                    hasR = (t0 + n + HALF) <= hi
                    if not hasL:
                        DVE("memset", (Gs[:, :, 0:HALF], 0.0), [], [("Gs",)])
                    if not hasR:
                        DVE("memset", (Gs[:, :, HALF + n:HALF + n + HALF], 0.0), [], [("Gs",)])
                    hkeys = [("H", ti)]
                    if hasL:
                        hkeys.append(("H", ti - 1))
                    if hasR:
                        hkeys.append(("H", ti + 1))
                    w0 = t0 - HALF if hasL else t0
                    w1 = t0 + n + HALF if hasR else t0 + n
                    g0 = 0 if hasL else HALF
                    W = w1 - w0
                    nm = min(W, 512)
                    nr = W - nm
                    for sidx in range(4):
                        sl, sk = get_slab(wsl(pw1[i2], sidx * 512, (sidx + 1) * 512), 8)
                        for cc in range(2):
                            c = 2 * sidx + cc
                            pa = cc
                            pg = 2 + cc
                            hb = 4 if cc == 0 else 7
                            for (which, colo, pbank) in ((0, cc * 128, pa), (1, 256 + cc * 128, pg)):
                                for k in range(8):
                                    MM(PS[pbank][:, 0:nm], sl[:, k, colo:colo + 128], H[:, k, w0:w0 + nm], k == 0, k == 7,
                                       reads=[sk] + hkeys, writes=[psk(pbank)], last=(k == 7))
                                if nr:
                                    ho = cc * 128 + which * 32
                                    for k in range(8):
                                        MM(PS[hb][:, ho:ho + nr], sl[:, k, colo:colo + 128], H[:, k, w0 + nm:w1], k == 0, k == 7,
                                           reads=[sk] + hkeys, writes=[psk(hb)], last=(k == 7))
                            sb_ = cnt["sg"] % len(sig)
                            cnt["sg"] += 1
                            ACT(sig[sb_][:, 0:nm], PS[pg][:, 0:nm], AF.Sigmoid, [psk(pg)], [("sig", sb_)])
                            TT(Gs[:, c, g0:g0 + nm], PS[pa][:, 0:nm], sig[sb_][:, 0:nm], ALU.mult, [psk(pa), ("sig", sb_)], [("Gs",)])
                            if nr:
                                ho = cc * 128
                                ACT(sigh[sb_][:, 0:nr], PS[hb][:, ho + 32:ho + 32 + nr], AF.Sigmoid, [psk(hb)], [("sigh", sb_)])
                                TT(Gs[:, c, g0 + nm:g0 + nm + nr], PS[hb][:, ho:ho + nr], sigh[sb_][:, 0:nr], ALU.mult,
                                   [psk(hb), ("sigh", sb_)], [("Gs",)])
                        if hook is not None:
                            hook(sidx)

                def seg_conv(seg):
                    (t0, n, s, ti, lo, hi) = seg
                    def build_dg(c):
                        db = c % 2
                        TT(dg[db][:, :, :], idb_b, dwt[:, c, :].unsqueeze(2).broadcast_to([128, CONVW, 128]), ALU.mult,
                           [("c", "idb"), ("dwt",)], [("dg", db)])

                    def stats_mm(c):
                        MM(PS[7][:, 0:n], ONES[:, :], ybf[0][:, 0:n], c == 0, c == 7,
                           reads=[("ybf", 0), ("c", "ones")], writes=[psk(7)], last=True)
                        MM(PS[4][:, 0:n], ONES[:, :], ysq[0][:, 0:n], c == 0, c == 7,
                           reads=[("ysq", 0), ("c", "ones")], writes=[psk(4)], last=True)

                    if not cnt["dg"]:
                        build_dg(0)
                        build_dg(1)
                        cnt["dg"] = 1
                    for c in range(8):
                        db = c % 2
                        py = 5 + (c % 2)
                        for k in range(CONVW):
                            MM(PS[py][:, 0:n], dg[db][:, k, :], Gs[:, c, k:k + n], k == 0, k == CONVW - 1,
                               reads=[("dg", db), ("Gs",)], writes=[psk(py)], last=(k == CONVW - 1))
                        if c >= 1:
                            stats_mm(c - 1)
                        ACT(yb[:, c, 0:n], PS[py][:, 0:n], AF.Identity, [psk(py)] + VK, [("yb", c)],
                            bias=vec("bdw", i2 * 8 + c, i2 * 8 + c + 1), scale=1.0)
                        ACT(ysq[0][:, 0:n], yb[:, c, 0:n], AF.Square, [("yb", c)], [("ysq", 0)])
                        DVE("tensor_copy", (ybf[0][:, 0:n], yb[:, c, 0:n]), [("yb", c)], [("ybf", 0)])
                        build_dg((c + 2) % 8)
                    stats_mm(7)

                def ln_head(seg):
                    (t0, n, s, ti, lo, hi) = seg
                    ACT(mean_s[:, 0:n], PS[7][:, 0:n], AF.Copy, [psk(7)], [("mean",)])
                    TT(var_s[:, 0:n], mean_s[:, 0:n], mean_s[:, 0:n], ALU.mult, [("mean",)], [("var",)])
                    STT(var_s[:, 0:n], var_s[:, 0:n], -1.0, PS[4][:, 0:n], ALU.mult, ALU.add, [("var",), psk(4)], [("var",)])
                    DVE("tensor_scalar_max", (var_s[:, 0:n], var_s[:, 0:n], 0.0), [("var",)], [("var",)])
                    ACT(var_s[:, 0:n], var_s[:, 0:n], AF.Ln, [("var",), ("c", "eps")], [("var",)], bias=EPST[:, 0:1], scale=1.0)
                    ACT(var_s[:, 0:n], var_s[:, 0:n], AF.Exp, [("var",)], [("var",)], scale=-0.5)

                def ln_piece(seg, c):
                    (t0, n, s, ti, lo, hi) = seg
                    TT(yb[:, c, 0:n], yb[:, c, 0:n], mean_s[:, 0:n], ALU.subtract, [("yb", c), ("mean",)], [("yb", c)])
                    TT(yb[:, c, 0:n], yb[:, c, 0:n], var_s[:, 0:n], ALU.mult, [("yb", c), ("var",)], [("yb", c)])
                    ACT(Zs[:, c, 0:n], yb[:, c, 0:n], AF.Silu, [("yb", c)] + VK, [("Zs", c)],
                        bias=vec("lnb", i2 * 8 + c, i2 * 8 + c + 1), scale=vec("lng", i2 * 8 + c, i2 * 8 + c + 1))

                def seg_pw2(seg):
                    (t0, n, s, ti, lo, hi) = seg
                    slabs = [get_slab(wsl(pw2[i2], 0, 512), 8), get_slab(wsl(pw2[i2], 512, 1024), 8)]
                    proj_residual(slabs, Zs, "Zs", n, t0, s, ti, [0, 1, 2], rot)

                lb = l % 2
                tl = lambda sg: (sg[0], sg[1], sg[2], sg[3])
                seg_pw1(segs[0])
                flush_deferred()
                seg_conv(segs[0])
                for si in range(len(segs)):
                    ln_head(segs[si])
                    if si + 1 < len(segs):
                        seg_pw1(segs[si + 1], hook=(lambda si_: (lambda sidx: [ln_piece(segs[si_], c_) for c_ in ((0, 1, 2), (3, 4, 5), (6, 7), ())[sidx]]))(si))
                    else:
                        for c in range(8):
                            ln_piece(segs[si], c)
                    if si >= 1:
                        norm_A(tl(segs[si - 1]))
                    seg_pw2(segs[si])
                    if si >= 1:
                        norm_B(lb, 1, tl(segs[si - 1]), 3)
                    if si + 1 < len(segs):
                        seg_conv(segs[si + 1])
                norm_A(tl(segs[-1]))
                deferred.append((lambda lb_, tl_: (lambda: norm_B(lb_, 1, tl_, 7)))(lb, tl(segs[-1])))

            def even_mixer(l, segs):
                i2 = l // 2
                Sd.barrier()
                A.reset()
                NPB = 10
                KT = A.bf16(NT)
                V = A.bf16(18, 128)
                Qs = A.bf16(4, 512)
                Us = A.bf16(6, 512)
                Pb = [A.bf16(512) for _ in range(NPB)]
                Db = A.bf16(4, 512)
                Ys = A.bf16(8, 512)
                BND = A.bf16(20, 128)
                MSK = A.bf16(2, 128)
                PW = A.bf16(4, 128)
                rc = [A.f32(512) for _ in range(1)]
                rs = [A.f32(512) for _ in range(1)]
                t1 = A.f32(512)
                UB = [A.bf16(512) for _ in range(2)]
                PRM = A.bf16(128)
                SINK = A.f32(4)
                dn = [A.f32(512) for _ in range(1)]
                rot = {"i": 0}
                cnt = {"rope": 0, "p": 0, "dn": 0, "pv": 0}
                Sd.fence("pool")
                Sd.fence("sp")
                Sd.dma("pool", ch_constp, BND[:, :, :], bandsd.rearrange("p (a b) -> p a b", a=20), writes=[("BND",)])
                Sd.dma("pool", ch_constp, MSK[:, :, :], masksd.rearrange("p (a b) -> p a b", a=2), writes=[("MSK",)])
                Sd.dma("pool", ch_constp, PW[:, :, :], pool_w[i2].rearrange("g c d -> c g d"), writes=[("PW",)])
                Sd.dma("pool", ch_constp, PRM[:, :], permd, writes=[("PRM",)])
                setw([("BND",), ("MSK",), ("PW",), ("PRM",)], ch_constp)
                ACT(SINK[:, 0:4], vec("sinkc", i2 * 4, i2 * 4 + 4), AF.Exp, VK, [("SINK",)])

                def load_rope(t0, n):
                    b = cnt["rope"] % len(rc)
                    cnt["rope"] += 1
                    Sd.dma("sp", ch_rope[b], rc[b][:, 0:n], ropec[:, t0:t0 + n], writes=[("rc", b)])
                    Sd.dma("sp", ch_rope[b], rs[b][:, 0:n], ropes[:, t0:t0 + n], writes=[("rs", b)])
                    setw([("rc", b)], ch_rope[b])
                    return b

                ucnt = {"i": 0}

                def rope_a(pq, rb_, n):
                    ub = ucnt["i"] % 2
                    ucnt["i"] += 1
                    TT(t1[:, 0:n], PS[pq][:, 0:n], rc[rb_][:, 0:n], ALU.mult, [psk(pq), ("rc", rb_)], [("t1",)])
                    TT(UB[ub][:, 0:n], PS[pq][:, 0:n], rs[rb_][:, 0:n], ALU.mult, [psk(pq), ("rs", rb_)], [("UB", ub)])
                    return ub

                def rope_b(pq2, ub, n):
                    MM(PS[pq2][:, 0:n], PRM[:, :], UB[ub][:, 0:n], True, True, reads=[("PRM",), ("UB", ub)], writes=[psk(pq2)], last=True)

                def rope_c(pq2, n, dst, dkey):
                    TT(dst, t1[:, 0:n], PS[pq2][:, 0:n], ALU.add, [("t1",), psk(pq2)], [dkey])

                kvs, kvk = get_slab(wsl(w_in[i2], 0, 256), 8)
                for it, (t0, n, s, ti) in enumerate(x_tiles(True)):
                    rb_ = load_rope(t0, n)
                    for k in range(8):
                        MM(PS[0][:, 0:n], kvs[:, k, 0:128], H[:, k, t0:t0 + n], k == 0, k == 7,
                           reads=[kvk, ("H", ti)], writes=[psk(0)], last=(k == 7))
                    ub = rope_a(0, rb_, n)
                    nb = n // 128
                    pv = 2 + (it % 2)
                    for bl in range(nb):
                        for k in range(8):
                            MM(PS[pv][:, bl * 128:(bl + 1) * 128], H[:, k, t0 + bl * 128:t0 + (bl + 1) * 128], kvs[:, k, 128:256], k == 0, k == 7,
                               reads=[kvk, ("H", ti)], writes=[psk(pv)], last=(k == 7))
                    rope_b(1, ub, n)
                    rope_c(1, n, KT[:, t0:t0 + n], ("KT",))
                    b0 = t0 // 128
                    ACT(V[:, b0:b0 + nb, :], PS[pv][:, 0:nb * 128].rearrange("p (b d) -> p b d", b=nb), AF.Copy, [psk(pv)], [("V",)])
                    if it == 0:
                        flush_deferred()

                prev_tile = [None]
                for (t0, n, s, ti, lo, hi) in segs:
                    is_ctx = (s == 1)
                    nqb = n // 128
                    n0 = t0 // 128
                    rb_ = load_rope(t0, n)
                    qs_, qk_ = get_slab(wsl(w_in[i2], 256, 768), 8)

                    def qmm(c):
                        pq = (c % 2) * 2
                        for k in range(8):
                            MM(PS[pq][:, 0:n], qs_[:, k, c * 128:(c + 1) * 128], H[:, k, t0:t0 + n], k == 0, k == 7,
                               reads=[qk_, ("H", ti)], writes=[psk(pq)], last=(k == 7))

                    ubs = {}
                    qmm(0)
                    ubs[0] = rope_a(0, rb_, n)
                    for c in range(4):
                        pq = (c % 2) * 2
                        if c + 1 < 4:
                            qmm(c + 1)
                        rope_b(pq + 1, ubs[c], n)
                        rope_c(pq + 1, n, Qs[:, c, 0:n], ("Qs",))
                        if c + 1 < 4:
                            ubs[c + 1] = rope_a(((c + 1) % 2) * 2, rb_, n)
                    def att_scores(g, nl):
                        gs = slice(g * 64, (g + 1) * 64)
                        nblk = n0 + nl
                        kbs = [(16, None), (17, None)]
                        if not is_ctx:
                            for m, mk in ((nblk - 1, 0), (nblk, None), (nblk + 1, 1)):
                                if 0 <= m <= 15:
                                    kbs.append((m, mk))
                        ptiles = []
                        for (m, mk) in kbs:
                            sbk = cnt["p"] % 4
                            pi = cnt["p"] % NPB
                            cnt["p"] += 1
                            MM(PS[sbk][:, :], KT[gs, m * 128:(m + 1) * 128], Qs[gs, :, nl * 128:(nl + 1) * 128], True, True,
                               reads=[("KT",), ("Qs",)], writes=[psk(sbk)], last=True)
                            ACT(Pb[pi][:, :], PS[sbk][:, :], AF.Exp, [psk(sbk)], [("P", pi)], scale=0.125)
                            if mk is not None:
                                p3 = Pb[pi][:, :].rearrange("p (c q) -> p c q", c=4)
                                TT(p3, p3, MSK[:, mk, :].unsqueeze(1).broadcast_to([128, 4, 128]), ALU.mult, [("P", pi), ("MSK",)], [("P", pi)])
                            ptiles.append((m, pi))
                        return ptiles

                    def att_pv(g, nl, ptiles):
                        gs = slice(g * 64, (g + 1) * 64)
                        pn = 4 + 2 * (cnt["pv"] % 2)
                        pd = pn + 1
                        cnt["pv"] += 1
                        np_ = len(ptiles)
                        for ii, (m, pi) in enumerate(ptiles):
                            MM(PS[pn][:, :], V[:, m, :], Pb[pi][:, :], ii == 0, ii == np_ - 1,
                               reads=[("V",), ("P", pi)], writes=[psk(pn)], last=(ii == np_ - 1))
                        for ii, (m, pi) in enumerate(ptiles):
                            MM(PS[pd][:, :], ONE1[:, :], Pb[pi][:, :], ii == 0, ii == np_ - 1,
                               reads=[("c", "one1"), ("P", pi)], writes=[psk(pd)], last=(ii == np_ - 1))
                        dk = ("dn", g)
                        TT(dn[0][gs, :].rearrange("p (c q) -> p c q", c=4), PS[pd][gs, :].rearrange("p (c q) -> p c q", c=4),
                           SINK[gs, 0:4].unsqueeze(2).broadcast_to([64, 4, 128]), ALU.add, [psk(pd), ("SINK",)], [dk])
                        ACT(dn[0][gs, :], dn[0][gs, :], AF.Ln, [dk], [dk])
                        ACT(dn[0][gs, :], dn[0][gs, :], AF.Exp, [dk], [dk], scale=-1.0)
                        return (g, nl, pn)

                    def att_fin(g, nl, pn):
                        gs = slice(g * 64, (g + 1) * 64)
                        TT(Ys[gs, 4:8, nl * 128:(nl + 1) * 128], PS[pn][gs, :].rearrange("p (c q) -> p c q", c=4),
                           dn[0][gs, :].rearrange("p (c q) -> p c q", c=4), ALU.mult, [psk(pn), ("dn", g)], [("Ys",)])

                    prevg = None
                    prevf = None
                    for nl in range(nqb):
                        for g in range(2):
                            pt = att_scores(g, nl)
                            if prevg is not None:
                                f = att_pv(*prevg)
                                if prevf is not None:
                                    att_fin(*prevf)
                                prevf = f
                            prevg = (g, nl, pt)
                    f = att_pv(*prevg)
                    if prevf is not None:
                        att_fin(*prevf)
                    att_fin(*f)
                    us_, uk_ = get_slab(wsl(w_in[i2], 768, 1280), 8)
                    sb0 = lo // 128
                    sb1 = hi // 128 - 1
                    blks = [b for b in range(n0 - 1, n0 + nqb + 1) if sb0 <= b <= sb1]
                    for ui, b in enumerate(blks):
                        pu = ui % 2
                        tix = min(b // 4, 4)
                        for k in range(8):
                            MM(PS[pu][:, :], H[:, k, b * 128:(b + 1) * 128], us_[:, k, :], k == 0, k == 7,
                               reads=[uk_, ("H", tix)], writes=[psk(pu)], last=(k == 7))
                        ACT(Us[:, ui, :], PS[pu][:, :], AF.Copy, [psk(pu)], [("Us", ui)])
                    for g4 in range(4):
                        pdd = 2 + (g4 % 2)
                        for nl in range(nqb):
                            b = n0 + nl
                            srcs = []
                            for m in (b - 1, b, b + 1):
                                if m < sb0 or m > sb1:
                                    continue
                                if m == b:
                                    var_ = 0 if b == sb0 else (2 if b == sb1 else 1)
                                else:
                                    var_ = 3 if m == b - 1 else 4
                                srcs.append((blks.index(m), var_))
                            for ii, (ui, var_) in enumerate(srcs):
                                MM(PS[pdd][:, nl * 128:(nl + 1) * 128], Us[:, ui, g4 * 128:(g4 + 1) * 128], BND[:, g4 * 5 + var_, :],
                                   ii == 0, ii == len(srcs) - 1,
                                   reads=[("Us", ui), ("BND",)], writes=[psk(pdd)], last=(ii == len(srcs) - 1))
                        DVE("tensor_copy", (Db[:, g4, 0:n], PS[pdd][:, 0:n]), [psk(pdd)], [("Db", g4)])
                        pp = 6 + (g4 % 2)
                        MM(PS[pp][:, 0:n], PW[:, g4, :], Db[:, g4, 0:n], True, True,
                           reads=[("PW",), ("Db", g4)], writes=[psk(pp)], last=True)
                        ACT(Ys[:, g4, 0:n], PS[pp][:, 0:n], AF.Identity, [psk(pp)] + VK, [("Ys",)],
                            scale=vec("pscale", i2 * 4 + g4, i2 * 4 + g4 + 1))
                    if prev_tile[0] is not None:
                        norm_A(prev_tile[0])
                    slabs = [get_slab(wsl(w_out[i2], 0, 512), 8), get_slab(wsl(w_out[i2], 512, 1024), 8)]
                    proj_residual(slabs, Ys, ("Ys",), n, t0, s, ti, [0, 1, 2, 3], rot)
                    if prev_tile[0] is not None:
                        norm_B(l % 2, 1, prev_tile[0], 7)
                    prev_tile[0] = (t0, n, s, ti)
                norm_A(prev_tile[0])
                deferred.append((lambda lb_, tl_: (lambda: norm_B(lb_, 1, tl_, 7)))(l % 2, prev_tile[0]))
                if l == 0 and debug:
                    o_ = 8 * NT
                    dump(dbgb[:, o_:o_ + NT], KT[:, :], [("KT",)])
                    o_ += NT
                    dump(dbgb[:, o_:o_ + 18 * 128].rearrange("p (a b) -> p a b", a=18), V[:, :, :], [("V",)])
                    o_ += 18 * 128
                    dump(dbgb[:, o_:o_ + 8 * 512].rearrange("p (a b) -> p a b", a=8), Ys[:, :, :], [("Ys",)])
                    o_ += 8 * 512
                    dump(dbgb[:, o_:o_ + 4 * 512].rearrange("p (a b) -> p a b", a=4), Qs[:, :, :], [("Qs",)])
                    o_ += 4 * 512
                    dump(dbgb[:, o_:o_ + 6 * 512].rearrange("p (a b) -> p a b", a=6), Us[:, :, :], [("Us", i) for i in range(6)])

            ch_dbg = Sd.chan("dbg")

            def dump(dst, src, keys):
                if not debug:
                    return
                Sd.dma("sp", ch_dbg, dst, src, reads=keys)
                if not dry:
                    for nm in ("pe", "act", "dve"):
                        Sd.E[nm].q.append(("wait", ch_dbg.sem, ch_dbg.count))

            A.reset()
            mod_ring_alloc(False, 4)
            rst5 = [A.f32(512) for _ in range(5)]
            for it0, tile0 in enumerate(x_tiles(True)):
                norm_A(tile0)
                norm_B_stats(tile0, it0 % 2, rst5[it0])
            for j in range(4):
                mod_slab(0, j)
            mod_finish(0, 0)
            for it0, tile0 in enumerate(x_tiles(True)):
                norm_B_mod(0, 0, tile0, rst5[it0])
            for j in range(4, 12):
                mod_slab(0, j)
            mod_finish(0, 1)
            for l in range(n_layers):
                cur["lb"] = l % 2
                upd_ctx = l < 2
                segs = [(i * 512, 512, 0, i, 0, S) for i in range(4)]
                if upd_ctx:
                    segs.append((S, L, 1, 4, S, NT))
                if l == 0 and debug:
                    dump(dbgf[:, 16 * NT:16 * NT + 96], MODV2[0][:, :], [("modv", 0)])
                    dump(dbgf[:, 16 * NT + 96:16 * NT + 160], AB2[0][:, :], [("ab", 0)])
                    dump(dbgb[:, 0:8 * NT].rearrange("p (a b) -> p a b", a=8), H[:, :, :], [("H", i) for i in range(5)])
                if l % 2 == 0:
                    even_mixer(l, segs)
                else:
                    conv_mixer(l, segs)
                if l == 0 and debug:
                    dump(dbgf[:, 0:8 * NT].rearrange("p (a b) -> p a b", a=8), X[:, :, :], [("X", i) for i in range(5)])
                nl_ = (l + 1) if (l + 1) < n_layers else None
                is_final = (nl_ is None) and final_norm
                if nl_ is not None:
                    next_tiles = x_tiles(nl_ <= 2)
                elif is_final:
                    next_tiles = x_tiles(False)
                else:
                    next_tiles = []
                mlp_phase(l, x_tiles(upd_ctx), nl_, next_tiles, is_final)
                if l == 0 and debug:
                    dump(dbgf[:, 8 * NT:16 * NT].rearrange("p (a b) -> p a b", a=8), X[:, :, :], [("X", i) for i in range(5)])
            flush_deferred()
            if not final_norm:
                Sd.barrier()
                Sd.fence("sp")
                for (t0, n, s, ti) in x_tiles(False):
                    Sd.dma("sp", out_chan(("x", ti)), outT[:, t0:t0 + n].rearrange("(c p) t -> p c t", p=128), X[:, :, t0:t0 + n], reads=[("X", ti)])
            if not dry:
                for ch_ in ch_out.values():
                    Sd.E["sp"].q.append(("wait", ch_.sem, ch_.count))

        plan = []
        emit(Sched(nc, st, dry=True), plan)
        Sreal = Sched(nc, st, dry=False)
        emit(Sreal, plan)
        with nc.Block() as block:
            Sreal.flush(block)
    return nc


def _chunkT(v):
    v = np.asarray(v, np.float32).reshape(-1, 128)
    return np.ascontiguousarray(v.T)


def _consts():
    ident = np.eye(128, dtype=np.float32)
    bands = np.zeros((4, 5, 128, 128), np.float32)
    Sx = 384
    t = np.arange(Sx)
    for g, w in enumerate((2, 4, 8, 16)):
        lo = np.clip(t - w // 2, 0, Sx)
        hi = np.clip(t + w - w // 2, 0, Sx)
        cntv = (hi - lo).astype(np.float32)
        M = np.zeros((Sx, Sx), np.float32)
        for to in range(Sx):
            M[lo[to]:hi[to], to] = np.float32(1.0) / cntv[to]
            M[to, to] -= 1.0
        blk = lambda bi, bo: M[bi * 128:(bi + 1) * 128, bo * 128:(bo + 1) * 128]
        bands[g, 0] = blk(0, 0)
        bands[g, 1] = blk(1, 1)
        bands[g, 2] = blk(2, 2)
        bands[g, 3] = blk(0, 1)
        bands[g, 4] = blk(2, 1)
    bands = np.ascontiguousarray(bands.reshape(20, 128, 128).transpose(1, 0, 2).reshape(128, 20 * 128))
    j = np.arange(128)[:, None]
    q = np.arange(128)[None, :]
    m0 = (j >= q).astype(np.float32)
    m1 = (j <= q).astype(np.float32)
    masks = np.concatenate([m0, m1], axis=1)
    inv = (10000.0 ** (-np.arange(0, 32, 2, dtype=np.float32) / np.float32(32))).astype(np.float32)
    tt = np.arange(S)
    row = (tt // 64).astype(np.float32)
    col = (tt % 64).astype(np.float32)
    rc = np.ones((128, NT), np.float32)
    rs = np.zeros((128, NT), np.float32)
    for p in range(128):
        d = p % 64
        blk_ = d // 32
        r = d % 32
        ang = (row if blk_ == 0 else col) * inv[r % 16]
        rc[p, :S] = np.cos(ang.astype(np.float32))
        sn = np.sin(ang.astype(np.float32))
        rs[p, :S] = -sn if r < 16 else sn
    pidx = np.arange(128)
    dd = pidx % 64
    pp = (pidx // 64) * 64 + (dd // 32) * 32 + np.where(dd % 32 < 16, dd % 32 + 16, dd % 32 - 16)
    perm = np.zeros((128, 128), np.float32)
    perm[pp, pidx] = 1.0
    rs = np.ascontiguousarray(rs[pp, :])
    return ident, bands, np.ascontiguousarray(masks), rc, rs, perm


def _prep_shared(inp):
    f = lambda a: np.ascontiguousarray(np.asarray(a, np.float32))
    sh = {}
    sh["w_mod"] = f(inp["w_mod"])
    sh["bmod"] = np.concatenate([_chunkT(inp["b_mod"][l]) for l in range(DEPTH)], axis=1)
    sh["n1g"] = np.concatenate([_chunkT(inp["norm1_g"][l]) for l in range(DEPTH)], axis=1)
    sh["n2g"] = np.concatenate([_chunkT(inp["norm2_g"][l]) for l in range(DEPTH)], axis=1)
    d = np.arange(64)
    partner = (d // 32) * 32 + np.where(d % 32 < 16, d % 32 + 16, d % 32 - 16)
    kcols = np.concatenate([1024 + g * 64 + d for g in range(2)])
    kpcols = np.concatenate([1024 + g * 64 + partner for g in range(2)])
    vcols = np.arange(1152, 1280)
    qcols = np.concatenate([np.concatenate([512 + (g * 4 + c) * 64 + d for g in range(2)]) for c in range(4)])
    qpcols = np.concatenate([np.concatenate([512 + (g * 4 + c) * 64 + partner for g in range(2)]) for c in range(4)])
    ucols = np.arange(0, 512)
    cols = np.concatenate([kcols, vcols, qcols, ucols])
    sh["w_in"] = f(np.asarray(inp["mix_w_in"])[:, :, cols])
    sh["pool_w"] = f(inp["pool_w"])
    sh["pscale"] = np.concatenate([_chunkT(inp["pool_scale"][i]) for i in range(2)], axis=1)
    p = np.arange(128)
    sh["sinkc"] = np.concatenate([np.stack([np.asarray(inp["attn_sink"], np.float32)[i][(p // 64) * 4 + c] for c in range(4)], axis=1)
                                  for i in range(2)], axis=1).astype(np.float32)
    arows = np.concatenate([np.concatenate([512 + (g * 4 + c) * 64 + d for g in range(2)]) for c in range(4)])
    rows = np.concatenate([np.arange(512), arows])
    sh["w_out"] = f(np.asarray(inp["mix_w_out"])[:, rows, :])
    pcols = np.concatenate([np.concatenate([np.arange(2 * s_ * 128, 2 * s_ * 128 + 256), np.arange(1024 + 2 * s_ * 128, 1024 + 2 * s_ * 128 + 256)])
                            for s_ in range(4)])
    sh["pw1"] = f(np.asarray(inp["conv_w_pw1"])[:, :, pcols])
    dw = np.asarray(inp["conv_w_dw"], np.float32)
    sh["dwT"] = np.concatenate([np.ascontiguousarray(dw[i].reshape(31, 8, 128).transpose(2, 1, 0)).reshape(128, 248) for i in range(2)], axis=1)
    sh["bdw"] = np.concatenate([_chunkT(inp["conv_b_dw"][i]) for i in range(2)], axis=1)
    sh["lng"] = np.concatenate([_chunkT(inp["conv_ln_g"][i]) for i in range(2)], axis=1)
    sh["lnb"] = np.concatenate([_chunkT(inp["conv_ln_b"][i]) for i in range(2)], axis=1)
    sh["pw2"] = f(inp["conv_w_pw2"])
    sh["w1"] = f(inp["mlp_w1"])
    sh["w2"] = f(inp["mlp_w2"])
    sh["fing"] = _chunkT(inp["final_g"])
    ident, bands, masks, rc, rs, perm = _consts()
    sh["permc"] = perm
    sh["ident"] = ident
    sh["bands"] = bands
    sh["masks"] = masks
    sh["ropec"] = rc
    sh["ropes"] = rs
    return {k: np.ascontiguousarray(v, dtype=np.float32) for k, v in sh.items()}


def run(inputs, n_layers=DEPTH, final_norm=True, cores=8, trace=False, debug=False):
    x = np.asarray(inputs["x"], np.float32)
    ctx = np.asarray(inputs["ctx"], np.float32)
    c = np.asarray(inputs["c"], np.float32)
    c_ctx = np.asarray(inputs["c_ctx"], np.float32)
    sh = _prep_shared(inputs)
    in_maps = []
    for b in range(cores):
        m = dict(sh)
        m["xT"] = np.ascontiguousarray(np.concatenate([x[b].T, ctx[b].T], axis=1))
        cv = np.stack([_chunkT(c[b]), _chunkT(c_ctx)], axis=2).reshape(128, 16)
        m["cvec"] = np.ascontiguousarray(cv)
        in_maps.append(m)
    nc = build(n_layers, final_norm, debug)
    res = run_bass_kernel_spmd(nc, in_maps, core_ids=list(range(cores)), **({"trace": True} if trace else {}))
    out = np.stack([np.ascontiguousarray(r["outT"].T) for r in res.results], axis=0)
    return out.astype(np.float32), res


def kernel(**inputs):
    out, _ = run(inputs)
    return out
```

```python
import numpy as np
from contextlib import ExitStack
import concourse.bass as bass
import concourse.mybir as mybir
from concourse.bass_utils import run_bass_kernel_spmd

F32 = mybir.dt.float32
BF16 = mybir.dt.bfloat16
ALU = mybir.AluOpType
AF = mybir.ActivationFunctionType

D = 1024
S = 2048
L = 256
NT = S + L
DEPTH = 4
EPS = 1e-6
NSLOT = 4
CONVW = 31
HALF = 15


class Eng:
    def __init__(self, name, sem):
        self.name = name
        self.sem = sem
        self.count = 0
        self.q = []
        self.seen = {}


class Sched:
    def __init__(self, nc, stack, dry=False):
        self.nc = nc
        self.stack = stack
        self.dry = dry
        self.E = {}
        for n in ("pe", "act", "dve", "pool", "sp"):
            self.E[n] = Eng(n, None if dry else stack.enter_context(nc.semaphore("s_" + n)))
        self.lastw = {}
        self.lastr = {}
        self.nchan = 0

    def chan(self, name):
        return Eng(name, None if self.dry else self.stack.enter_context(self.nc.semaphore("c_" + name)))

    def _deps(self, eng, reads, writes):
        deps = {}

        def add(e, c, kind):
            if e is eng:
                if eng.name == "pe":
                    return
            if e.name not in deps or deps[e.name][1] < c:
                deps[e.name] = (e, c)

        for key in reads:
            w = self.lastw.get(key)
            if w:
                add(w[0], w[1], "raw")
        for key in writes:
            w = self.lastw.get(key)
            if w:
                add(w[0], w[1], "waw")
            for (e, c) in self.lastr.get(key, {}).values():
                add(e, c, "war")
        return deps

    def _emit_waits(self, eng, deps):
        for (e, c) in deps.values():
            if eng.seen.get(e.name, 0) < c:
                eng.seen[e.name] = c
                eng.q.append(("wait", e.sem, c))

    def _record(self, eng, cnt, reads, writes):
        for key in reads:
            self.lastr.setdefault(key, {})[eng.name] = (eng, cnt)
        for key in writes:
            self.lastw[key] = (eng, cnt)
            self.lastr[key] = {}

    def op(self, engname, meth, args, kw=None, reads=(), writes=()):
        if self.dry:
            return
        eng = self.E[engname]
        self._emit_waits(eng, self._deps(eng, reads, writes))
        eng.count += 1
        eng.q.append(("op", meth, args, kw or {}, eng.sem, 1))
        self._record(eng, eng.count, reads, writes)

    def mm(self, out, lhsT, rhs, start, stop, reads=(), writes=(), last=True):
        if self.dry:
            return
        eng = self.E["pe"]
        self._emit_waits(eng, self._deps(eng, reads, writes))
        cnt = eng.count + 1
        kw = {"start": bool(start), "stop": bool(stop)}
        if last:
            eng.count = cnt
            eng.q.append(("op", "matmul", (out, lhsT, rhs), kw, eng.sem, 1))
        else:
            eng.q.append(("op", "matmul", (out, lhsT, rhs), kw, None, 0))
        self._record(eng, cnt, reads, writes)

    def dma(self, qname, chan, out, in_, reads=(), writes=()):
        if self.dry:
            return
        q = self.E[qname]
        self._emit_waits(q, self._deps(chan, reads, writes))
        chan.count += 16
        q.q.append(("op", "dma_start", (), {"out": out, "in_": in_}, chan.sem, 16))
        self._record(chan, chan.count, reads, writes)

    def fence(self, qname, names=("pe", "act", "dve")):
        if self.dry:
            return
        q = self.E[qname]
        for b in names:
            eb = self.E[b]
            if q.seen.get(eb.name, 0) < eb.count:
                q.seen[eb.name] = eb.count
                q.q.append(("wait", eb.sem, eb.count))

    def barrier(self, names=("pe", "act", "dve"), waiters=("act", "dve")):
        if self.dry:
            return
        for a in waiters:
            ea = self.E[a]
            for b in names:
                if a == b:
                    continue
                eb = self.E[b]
                if ea.seen.get(eb.name, 0) < eb.count:
                    ea.seen[eb.name] = eb.count
                    ea.q.append(("wait", eb.sem, eb.count))

    def flush(self, block):
        def run(e, q):
            for it in q:
                if it[0] == "wait":
                    e.wait_ge(it[1], it[2])
                else:
                    ins = getattr(e, it[1])(*it[2], **it[3])
                    if it[4] is not None:
                        ins.then_inc(it[4], it[5])

        E = self.E

        @block.tensor
        def _(e):
            run(e, E["pe"].q)

        @block.scalar
        def _(e):
            run(e, E["act"].q)

        @block.vector
        def _(e):
            run(e, E["dve"].q)

        @block.gpsimd
        def _(e):
            run(e, E["pool"].q)

        @block.sync
        def _(e):
            run(e, E["sp"].q)


class Arena:
    def __init__(self, t, nwords):
        self.t = t
        self.n = nwords
        self.off = 0

    def reset(self):
        self.off = 0

    def _take(self, nw):
        a = self.off
        self.off += nw
        assert self.off <= self.n, ("arena overflow", self.off, self.n)
        return self.t[:, a:a + nw]

    def f32(self, *free):
        n = int(np.prod(free))
        ap = self._take(n)
        if len(free) == 2:
            ap = ap.rearrange("p (a b) -> p a b", a=free[0])
        return ap

    def bf16(self, *free):
        n = int(np.prod(free))
        nw = (n + 1) // 2
        ap = self._take(nw).bitcast(BF16)[:, 0:n]
        if len(free) == 2:
            ap = ap.rearrange("p (a b) -> p a b", a=free[0])
        return ap


VEC_LAYOUT = [("n1g", 32), ("n2g", 32), ("bmod", 192), ("pscale", 8), ("sinkc", 8), ("bdw", 16),
              ("lng", 16), ("lnb", 16), ("fing", 8), ("cvec", 16)]


def x_tiles(with_ctx):
    t = [(i * 512, 512, 0, i) for i in range(4)]
    if with_ctx:
        t.append((S, L, 1, 4))
    return t


def build(n_layers=DEPTH, final_norm=True, debug=False):
    nc = bass.Bass("TRN2", target_bir_lowering=False)
    dt_in = {}

    def din(name, shape):
        dt_in[name] = nc.dram_tensor(name, list(shape), F32, kind="ExternalInput").ap()
        return dt_in[name]

    xT = din("xT", [D, NT])
    din("cvec", [128, 16])
    w_mod = din("w_mod", [DEPTH, D, 6 * D])
    din("bmod", [128, 192])
    din("n1g", [128, 32])
    din("n2g", [128, 32])
    w_in = din("w_in", [2, D, 1280])
    pool_w = din("pool_w", [2, 4, 128, 128])
    din("pscale", [128, 8])
    din("sinkc", [128, 8])
    w_out = din("w_out", [2, D, D])
    pw1 = din("pw1", [2, D, 2 * D])
    din("dwT", [128, 496])
    din("bdw", [128, 16])
    din("lng", [128, 16])
    din("lnb", [128, 16])
    pw2 = din("pw2", [2, D, D])
    w1 = din("w1", [DEPTH, D, 4 * D])
    w2 = din("w2", [DEPTH, 4 * D, D])
    din("fing", [128, 8])
    identd = din("ident", [128, 128])
    bandsd = din("bands", [128, 20 * 128])
    masksd = din("masks", [128, 2 * 128])
    ropec = din("ropec", [128, NT])
    ropes = din("ropes", [128, NT])
    permd = din("permc", [128, 128])
    outT = nc.dram_tensor("outT", [D, S], F32, kind="ExternalOutput").ap()
    if debug:
        dbgf = nc.dram_tensor("dbgf", [128, 8 * NT + 8 * NT + 96 + 64], F32, kind="ExternalOutput").ap()
        dbgb = nc.dram_tensor("dbgb", [128, 8 * NT + NT + 18 * 128 + 8 * 512 + 4 * 512 + 6 * 512], BF16, kind="ExternalOutput").ap()

    with ExitStack() as st:
        def SB(name, shape, dt):
            return st.enter_context(nc.sbuf_tensor(name, shape, dt))

        X = SB("X", [128, 8, NT], F32)
        H = SB("H", [128, 8, NT], BF16)
        WR = [SB("wr%d" % i, [128, 4096], BF16) for i in range(NSLOT)]
        nvec = sum(n for _, n in VEC_LAYOUT)
        VEC = SB("VEC", [128, nvec], F32)
        MODV2 = [SB("MODV%d" % i, [128, 96], F32) for i in range(2)]
        AB2 = [SB("AB%d" % i, [128, 64], F32) for i in range(2)]
        RST = SB("RST", [128, 512], F32)
        TMP = SB("TMP", [128, 2, 512], F32)
        SC = SB("SC", [128, 16], BF16)
        ONES = SB("ONES", [128, 128], BF16)
        ONE1 = SB("ONE1", [128, 128], BF16)
        IDB = SB("IDB", [128, 128], BF16)
        EPST = SB("EPST", [128, 1], F32)
        ARENA_W = 14884
        ART = SB("ARENA", [128, ARENA_W], F32)
        PS = [st.enter_context(nc.psum_tensor("ps%d" % i, [128, 512], F32)) for i in range(8)]

        voff = {}
        o = 0
        for nm, n in VEC_LAYOUT:
            voff[nm] = o
            o += n

        def vec(nm, a, b):
            return VEC[:, voff[nm] + a: voff[nm] + b]

        def emit(Sd, plan):
            dry = Sd.dry
            slab_state = {"issued": 0, "consumed": 0}
            slot_ch = [Sd.chan("w%d" % i) for i in range(NSLOT)]
            ch_const = Sd.chan("const")
            ch_constp = Sd.chan("constp")
            ch_x = [Sd.chan("x%d" % i) for i in range(5)]
            ch_rope = [Sd.chan("rope%d" % i) for i in range(2)]
            ch_out = {}

            def out_chan(key):
                if key not in ch_out:
                    ch_out[key] = Sd.chan("out%d" % len(ch_out))
                return ch_out[key]
            A = Arena(ART, ARENA_W)
            psk = lambda i: ("ps", i)

            def ACT(out, in_, func, reads, writes, **kw):
                Sd.op("act", "activation", (out, in_, func), kw, reads, writes)

            def TT(out, a, b, op, reads, writes):
                Sd.op("dve", "tensor_tensor", (out, a, b, op), None, reads, writes)

            def STT(out, in0, scalar, in1, op0, op1, reads, writes):
                Sd.op("dve", "scalar_tensor_tensor", (out, in0, scalar, in1, op0, op1), None, reads, writes)

            def DVE(meth, args, reads, writes):
                Sd.op("dve", meth, args, None, reads, writes)

            MM = Sd.mm

            def setw(keys, ch):
                if not dry:
                    for k in keys:
                        Sd.lastw[k] = (ch, ch.count)

            def slot_view(slot, nk, ncol):
                v = WR[slot][:, :].rearrange("p (k n) -> p k n", k=nk)
                if ncol != v.shape[2]:
                    v = v[:, :, 0:ncol]
                return v

            def issue_slab(i):
                src, nk = plan[i]
                slot = i % NSLOT
                Sd.dma("pool", slot_ch[slot], slot_view(slot, nk, src.shape[2]), src, writes=[("W", slot)])

            def get_slab(src, nk):
                if dry:
                    plan.append((src, nk))
                    i = len(plan) - 1
                else:
                    i = slab_state["consumed"]
                    assert plan[i][1] == nk and tuple(plan[i][0].shape) == tuple(src.shape)
                    while slab_state["issued"] < min(len(plan), i + NSLOT - 1):
                        issue_slab(slab_state["issued"])
                        slab_state["issued"] += 1
                    slab_state["consumed"] += 1
                slot = i % NSLOT
                return slot_view(slot, nk, src.shape[2]), ("W", slot)

            def wsl(w2d, c0, c1):
                return w2d[:, c0:c1].rearrange("(k p) n -> p k n", p=128)

            VK = [("vec", nm) for nm, _ in VEC_LAYOUT]
            for nm, n in VEC_LAYOUT:
                Sd.dma("sp", ch_const, vec(nm, 0, n), dt_in[nm], writes=[("vec", nm)])
            Sd.dma("pool", ch_constp, IDB[:, :], identd, writes=[("c", "idb")])
            setw(VK, ch_const)
            DVE("memset", (ONES[:, :], 1.0 / 1024.0), [], [("c", "ones")])
            DVE("memset", (ONE1[:, :], 1.0), [], [("c", "one1")])
            DVE("memset", (EPST[:, :], EPS), [], [("c", "eps")])
            ACT(SC[:, :], vec("cvec", 0, 16), AF.Silu, VK, [("c", "sc")])
            for (t0, n, s, ti) in x_tiles(True):
                Sd.dma("sp", ch_x[ti], X[:, :, t0:t0 + n], xT[:, t0:t0 + n].rearrange("(c p) t -> p c t", p=128), writes=[("X", ti)])

            ch_m = [Sd.chan("m%d" % i) for i in range(4)]
            mstate = {"ring": None, "issued": set()}

            def mod_ring_alloc(fence, nslots=2):
                mstate["ring"] = [A.bf16(8, 512) for _ in range(nslots)]
                if fence:
                    Sd.fence("pool")

            def mod_issue(l, j):
                slot = j % len(mstate["ring"])
                Sd.dma("pool", ch_m[slot], mstate["ring"][slot], wsl(w_mod[l], j * 512, (j + 1) * 512), writes=[("MW", slot)])
                mstate["issued"].add((l, j))

            def mod_slab(l, j):
                nsl = len(mstate["ring"])
                for jj_ in range(j, min(12, j + nsl)):
                    if (l, jj_) not in mstate["issued"]:
                        mod_issue(l, jj_)
                sl = mstate["ring"][j % nsl]
                wk = ("MW", j % nsl)
                for jj in range(4):
                    cc = j * 4 + jj
                    for k in range(8):
                        MM(PS[7][:, cc * 2:cc * 2 + 2], sl[:, k, jj * 128:(jj + 1) * 128], SC[:, k * 2:k * 2 + 2], k == 0, k == 7,
                           reads=[wk, ("c", "sc")], writes=[psk(7)], last=(k == 7))

            cur = {"lb": 0}

            def mod_finish(l, part=None):
                lb = l % 2
                mv = MODV2[lb][:, :].rearrange("p (j s) -> p j s", s=2)
                ab = AB2[lb][:, :].rearrange("p (n a c s) -> p n a c s", n=2, a=2, c=8)
                bm = vec("bmod", l * 48, (l + 1) * 48)
                pv = PS[7][:, 0:96].rearrange("p (j s) -> p j s", s=2)
                j0, j1 = {None: (0, 48), 0: (0, 16), 1: (16, 48)}[part]
                for s in range(2):
                    TT(mv[:, j0:j1, s], pv[:, j0:j1, s], bm[:, j0:j1], ALU.add, [psk(7)] + VK, [("modv", lb)])
                for nrm in ([0, 1] if part is None else [part]):
                    g = vec("n1g" if nrm == 0 else "n2g", l * 8, (l + 1) * 8)
                    for s in range(2):
                        sh = mv[:, (3 * nrm) * 8:(3 * nrm + 1) * 8, s]
                        scv = mv[:, (3 * nrm + 1) * 8:(3 * nrm + 2) * 8, s]
                        STT(ab[:, nrm, 0, :, s], scv, 1.0, g, ALU.add, ALU.mult, [("modv", lb)] + VK, [("ab", lb)])
                        DVE("tensor_copy", (ab[:, nrm, 1, :, s], sh), [("modv", lb)], [("ab", lb)])

            def abv(lb, nrm, a, c, s):
                o_ = ((nrm * 2 + a) * 8 + c) * 2 + s
                return AB2[lb][:, o_:o_ + 1]

            def gate(which, c, s):
                o_ = ((3 * which + 2) * 8 + c) * 2 + s
                return MODV2[cur["lb"]][:, o_:o_ + 1]

            def mkey():
                return ("modv", cur["lb"])

            def norm_A(tile):
                (t0, n, s, ti) = tile
                ACT(H[:, :, t0:t0 + n], X[:, :, t0:t0 + n], AF.Square, [("X", ti)], [("H", ti)])

            def norm_B_stats(tile, pb, rst):
                (t0, n, s, ti) = tile
                for c in range(8):
                    MM(PS[pb][:, 0:n], ONES[:, :], H[:, c, t0:t0 + n], c == 0, c == 7,
                       reads=[("H", ti), ("c", "ones")], writes=[psk(pb)], last=(c == 7))
                rk = ("rst", id(rst))
                ACT(rst[:, 0:n], PS[pb][:, 0:n], AF.Ln, [psk(pb), ("c", "eps")], [rk], bias=EPST[:, 0:1], scale=1.0)
                ACT(rst[:, 0:n], rst[:, 0:n], AF.Exp, [rk], [rk], scale=-0.5)

            def norm_B_mod(lb, nrm, tile, rst, outb=None, stt_eng="dve"):
                (t0, n, s, ti) = tile
                rk = ("rst", id(rst))
                for c in range(8):
                    if outb is not None:
                        STT(outb[:, c, 0:n], X[:, c, t0:t0 + n], vec("fing", c, c + 1), rst[:, 0:n], ALU.mult, ALU.mult,
                            [("X", ti), rk] + VK, [("outb", id(outb))])
                    else:
                        tb = c % 2
                        if stt_eng == "dve":
                            STT(TMP[:, tb, 0:n], X[:, c, t0:t0 + n], abv(lb, nrm, 0, c, s), rst[:, 0:n], ALU.mult, ALU.mult,
                                [("X", ti), rk, ("ab", lb)], [("tmp", tb)])
                            ACT(H[:, c, t0:t0 + n], TMP[:, tb, 0:n], AF.Identity, [("tmp", tb), ("ab", lb)], [("H", ti)],
                                bias=abv(lb, nrm, 1, c, s), scale=1.0)
                        else:
                            Sd.op("pool", "tensor_tensor", (TMP[:, tb, 0:n], X[:, c, t0:t0 + n], rst[:, 0:n], ALU.mult), None,
                                  [("X", ti), rk], [("tmp", tb)])
                            ACT(H[:, c, t0:t0 + n], TMP[:, tb, 0:n], AF.Identity, [("tmp", tb), ("ab", lb)], [("H", ti)],
                                bias=abv(lb, nrm, 1, c, s), scale=abv(lb, nrm, 0, c, s))
                if outb is not None:
                    Sd.dma("sp", out_chan(id(outb)), outT[:, t0:t0 + n].rearrange("(c p) t -> p c t", p=128), outb[:, :, 0:n],
                           reads=[("outb", id(outb))])

            RSTv = RST[:, :]

            def norm_B(lb, nrm, tile, pb, outb=None, stt_eng="dve"):
                norm_B_stats(tile, pb, RSTv)
                norm_B_mod(lb, nrm, tile, RSTv, outb, stt_eng)

            deferred = []

            def flush_deferred():
                while deferred:
                    deferred.pop(0)()

            def norm_seq(lb, nrm, tiles, pb):
                for tile in tiles:
                    norm_A(tile)
                    norm_B(lb, nrm, tile, pb)

            def mlp_phase(l, tiles, next_l, next_tiles, is_final):
                A.reset()
                rb = [A.f32(512) for _ in range(4)]
                hid = [A.bf16(4, 512) for _ in range(2)]
                outbs = [A.f32(8, 512) for _ in range(2)] if is_final else None
                if next_l is not None:
                    mod_ring_alloc(True)
                ps2 = [4, 5, 6]
                st2 = {"i": 0}
                modj = {"j": 0}
                next_ti = set(t[3] for t in next_tiles)

                def stage1(w1s, w1k, it, t0, n, ti):
                    hb = it % 2
                    for hc in range(4):
                        for k in range(8):
                            MM(PS[hc][:, 0:n], w1s[:, k, hc * 128:(hc + 1) * 128], H[:, k, t0:t0 + n], k == 0, k == 7,
                               reads=[w1k, ("H", ti)], writes=[psk(hc)], last=(k == 7))
                        ACT(rb[hc][:, 0:n], PS[hc][:, 0:n], AF.Relu, [psk(hc)], [("rb", hc)])
                        TT(hid[hb][:, hc, 0:n], rb[hc][:, 0:n], rb[hc][:, 0:n], ALU.mult, [("rb", hc)], [("hid", hb)])

                def stage2(w2s, w2k, it, t0, n, s, ti):
                    hb = it % 2
                    for oc in range(8):
                        pb = ps2[st2["i"] % 3]
                        st2["i"] += 1
                        for kk in range(4):
                            MM(PS[pb][:, 0:n], w2s[:, kk, oc * 128:(oc + 1) * 128], hid[hb][:, kk, 0:n], kk == 0, kk == 3,
                               reads=[w2k, ("hid", hb)], writes=[psk(pb)], last=(kk == 3))
                        STT(X[:, oc, t0:t0 + n], PS[pb][:, 0:n], gate(1, oc, s), X[:, oc, t0:t0 + n], ALU.mult, ALU.add,
                            [psk(pb), mkey(), ("X", ti)], [("X", ti)])

                pend = []

                def after_tile(it):
                    tile = tiles[it]
                    if pend:
                        doB(pend.pop(0))
                    if tile[3] in next_ti:
                        norm_A(tile)
                        pend.append(it)

                def doB(it):
                    tile = tiles[it]
                    if is_final:
                        norm_B(0, 0, tile, 7, outb=outbs[it % 2])
                    else:
                        norm_B(next_l % 2, 0, tile, 7, stt_eng="pool")

                for e_ in range(8):
                    last = (e_ == 7)
                    w1s, w1k = get_slab(wsl(w1[l], e_ * 512, (e_ + 1) * 512), 8)
                    w2s, w2k = get_slab(w2[l][e_ * 512:(e_ + 1) * 512, :].rearrange("(k p) n -> p k n", p=128), 4)
                    for it, (t0, n, s, ti) in enumerate(tiles):
                        stage1(w1s, w1k, it, t0, n, ti)
                        if it >= 1:
                            (pt0, pn, ps_, pti) = tiles[it - 1]
                            stage2(w2s, w2k, it - 1, pt0, pn, ps_, pti)
                            if last:
                                after_tile(it - 1)
                            if it == 1 and e_ == 0:
                                flush_deferred()
                            if it == 2 and next_l is not None and e_ < 6:
                                for _ in range(2):
                                    mod_slab(next_l, modj["j"])
                                    modj["j"] += 1
                    (pt0, pn, ps_, pti) = tiles[-1]
                    stage2(w2s, w2k, len(tiles) - 1, pt0, pn, ps_, pti)
                    if last:
                        after_tile(len(tiles) - 1)
                        while len(pend) > 1:
                            doB(pend.pop(0))
                        if pend:
                            deferred.append((lambda it_: (lambda: doB(it_)))(pend.pop(0)))
                    if next_l is not None and e_ == 6:
                        assert modj["j"] == 12
                        mod_finish(next_l)

            def proj_residual(slabs, ysrc, ykey, n, t0, s, ti, banks, rot):
                for oc in range(8):
                    pb = banks[rot["i"] % len(banks)]
                    rot["i"] += 1
                    sl, sk = slabs[oc // 4]
                    for k in range(8):
                        MM(PS[pb][:, 0:n], sl[:, k, (oc % 4) * 128:(oc % 4 + 1) * 128], ysrc[:, k, 0:n], k == 0, k == 7,
                           reads=[sk, ((ykey, k) if isinstance(ykey, str) else ykey)], writes=[psk(pb)], last=(k == 7))
                    STT(X[:, oc, t0:t0 + n], PS[pb][:, 0:n], gate(0, oc, s), X[:, oc, t0:t0 + n], ALU.mult, ALU.add,
                        [psk(pb), mkey(), ("X", ti)], [("X", ti)])

            def conv_mixer(l, segs):
                i2 = l // 2
                Sd.barrier()
                A.reset()
                GW = 512 + 2 * HALF
                Gs = A.bf16(8, GW)
                Zs = A.bf16(8, 512)
                dg = [A.bf16(CONVW, 128) for _ in range(2)]
                ybf = [A.bf16(512) for _ in range(1)]
                ysq = [A.bf16(512) for _ in range(1)]
                yb = A.f32(8, 512)
                sig = [A.f32(512) for _ in range(1)]
                sigh = [A.f32(32) for _ in range(1)]
                mean_s = A.f32(512)
                var_s = A.f32(512)
                rot = {"i": 0}
                dwt = A.f32(8, CONVW)
                Sd.fence("sp")
                Sd.dma("sp", ch_const, dwt, dt_in["dwT"][:, i2 * 248:(i2 + 1) * 248].rearrange("p (c k) -> p c k", c=8), writes=[("dwt",)])
                setw([("dwt",)], ch_const)
                cnt = {"sg": 0, "dg": 0, "yb": 0}
                idb_b = IDB[:, :].unsqueeze(1).broadcast_to([128, CONVW, 128])
                def seg_pw1(seg, hook=None):
                    (t0, n, s, ti, lo, hi) = seg
                    hasL = (t0 - HALF) >= lo
                    hasR = (t0 + n + HALF) <= hi
                    if not hasL:
                        DVE("memset", (Gs[:, :, 0:HALF], 0.0), [], [("Gs",)])
                    if not hasR:
                        DVE("memset", (Gs[:, :, HALF + n:HALF + n + HALF], 0.0), [], [("Gs",)])
                    hkeys = [("H", ti)]
                    if hasL:
                        hkeys.append(("H", ti - 1))
                    if hasR:
                        hkeys.append(("H", ti + 1))
                    w0 = t0 - HALF if hasL else t0
                    w1 = t0 + n + HALF if hasR else t0 + n
                    g0 = 0 if hasL else HALF
                    W = w1 - w0
                    nm = min(W, 512)
                    nr = W - nm
                    for sidx in range(4):
                        sl, sk = get_slab(wsl(pw1[i2], sidx * 512, (sidx + 1) * 512), 8)
                        for cc in range(2):
                            c = 2 * sidx + cc
                            pa = cc
                            pg = 2 + cc
                            hb = 4 if cc == 0 else 7
                            for (which, colo, pbank) in ((0, cc * 128, pa), (1, 256 + cc * 128, pg)):
                                for k in range(8):
                                    MM(PS[pbank][:, 0:nm], sl[:, k, colo:colo + 128], H[:, k, w0:w0 + nm], k == 0, k == 7,
                                       reads=[sk] + hkeys, writes=[psk(pbank)], last=(k == 7))
                                if nr:
                                    ho = cc * 128 + which * 32
                                    for k in range(8):
                                        MM(PS[hb][:, ho:ho + nr], sl[:, k, colo:colo + 128], H[:, k, w0 + nm:w1], k == 0, k == 7,
                                           reads=[sk] + hkeys, writes=[psk(hb)], last=(k == 7))
                            sb_ = cnt["sg"] % len(sig)
                            cnt["sg"] += 1
                            ACT(sig[sb_][:, 0:nm], PS[pg][:, 0:nm], AF.Sigmoid, [psk(pg)], [("sig", sb_)])
                            TT(Gs[:, c, g0:g0 + nm], PS[pa][:, 0:nm], sig[sb_][:, 0:nm], ALU.mult, [psk(pa), ("sig", sb_)], [("Gs",)])
                            if nr:
                                ho = cc * 128
                                ACT(sigh[sb_][:, 0:nr], PS[hb][:, ho + 32:ho + 32 + nr], AF.Sigmoid, [psk(hb)], [("sigh", sb_)])
                                TT(Gs[:, c, g0 + nm:g0 + nm + nr], PS[hb][:, ho:ho + nr], sigh[sb_][:, 0:nr], ALU.mult,
                                   [psk(hb), ("sigh", sb_)], [("Gs",)])
                        if hook is not None:
                            hook(sidx)

                def seg_conv(seg):
                    (t0, n, s, ti, lo, hi) = seg
                    def build_dg(c):
                        db = c % 2
                        TT(dg[db][:, :, :], idb_b, dwt[:, c, :].unsqueeze(2).broadcast_to([128, CONVW, 128]), ALU.mult,
                           [("c", "idb"), ("dwt",)], [("dg", db)])

                    def stats_mm(c):
                        MM(PS[7][:, 0:n], ONES[:, :], ybf[0][:, 0:n], c == 0, c == 7,
                           reads=[("ybf", 0), ("c", "ones")], writes=[psk(7)], last=True)
                        MM(PS[4][:, 0:n], ONES[:, :], ysq[0][:, 0:n], c == 0, c == 7,
                           reads=[("ysq", 0), ("c", "ones")], writes=[psk(4)], last=True)

                    if not cnt["dg"]:
                        build_dg(0)
                        build_dg(1)
                        cnt["dg"] = 1
                    for c in range(8):
                        db = c % 2
                        py = 5 + (c % 2)
                        for k in range(CONVW):
                            MM(PS[py][:, 0:n], dg[db][:, k, :], Gs[:, c, k:k + n], k == 0, k == CONVW - 1,
                               reads=[("dg", db), ("Gs",)], writes=[psk(py)], last=(k == CONVW - 1))
                        if c >= 1:
                            stats_mm(c - 1)
                        ACT(yb[:, c, 0:n], PS[py][:, 0:n], AF.Identity, [psk(py)] + VK, [("yb", c)],
                            bias=vec("bdw", i2 * 8 + c, i2 * 8 + c + 1), scale=1.0)
                        ACT(ysq[0][:, 0:n], yb[:, c, 0:n], AF.Square, [("yb", c)], [("ysq", 0)])
                        DVE("tensor_copy", (ybf[0][:, 0:n], yb[:, c, 0:n]), [("yb", c)], [("ybf", 0)])
                        build_dg((c + 2) % 8)
                    stats_mm(7)

                def ln_head(seg):
                    (t0, n, s, ti, lo, hi) = seg
                    ACT(mean_s[:, 0:n], PS[7][:, 0:n], AF.Copy, [psk(7)], [("mean",)])
                    TT(var_s[:, 0:n], mean_s[:, 0:n], mean_s[:, 0:n], ALU.mult, [("mean",)], [("var",)])
                    STT(var_s[:, 0:n], var_s[:, 0:n], -1.0, PS[4][:, 0:n], ALU.mult, ALU.add, [("var",), psk(4)], [("var",)])
                    DVE("tensor_scalar_max", (var_s[:, 0:n], var_s[:, 0:n], 0.0), [("var",)], [("var",)])
                    ACT(var_s[:, 0:n], var_s[:, 0:n], AF.Ln, [("var",), ("c", "eps")], [("var",)], bias=EPST[:, 0:1], scale=1.0)
                    ACT(var_s[:, 0:n], var_s[:, 0:n], AF.Exp, [("var",)], [("var",)], scale=-0.5)

                def ln_piece(seg, c):
                    (t0, n, s, ti, lo, hi) = seg
                    TT(yb[:, c, 0:n], yb[:, c, 0:n], mean_s[:, 0:n], ALU.subtract, [("yb", c), ("mean",)], [("yb", c)])
                    TT(yb[:, c, 0:n], yb[:, c, 0:n], var_s[:, 0:n], ALU.mult, [("yb", c), ("var",)], [("yb", c)])
                    ACT(Zs[:, c, 0:n], yb[:, c, 0:n], AF.Silu, [("yb", c)] + VK, [("Zs", c)],
                        bias=vec("lnb", i2 * 8 + c, i2 * 8 + c + 1), scale=vec("lng", i2 * 8 + c, i2 * 8 + c + 1))

                def seg_pw2(seg):
                    (t0, n, s, ti, lo, hi) = seg
                    slabs = [get_slab(wsl(pw2[i2], 0, 512), 8), get_slab(wsl(pw2[i2], 512, 1024), 8)]
                    proj_residual(slabs, Zs, "Zs", n, t0, s, ti, [0, 1, 2], rot)

                lb = l % 2
                tl = lambda sg: (sg[0], sg[1], sg[2], sg[3])
                seg_pw1(segs[0])
                flush_deferred()
                seg_conv(segs[0])
                for si in range(len(segs)):
                    ln_head(segs[si])
                    if si + 1 < len(segs):
                        seg_pw1(segs[si + 1], hook=(lambda si_: (lambda sidx: [ln_piece(segs[si_], c_) for c_ in ((0, 1, 2), (3, 4, 5), (6, 7), ())[sidx]]))(si))
                    else:
                        for c in range(8):
                            ln_piece(segs[si], c)
                    if si >= 1:
                        norm_A(tl(segs[si - 1]))
                    seg_pw2(segs[si])
                    if si >= 1:
                        norm_B(lb, 1, tl(segs[si - 1]), 3)
                    if si + 1 < len(segs):
                        seg_conv(segs[si + 1])
                norm_A(tl(segs[-1]))
                deferred.append((lambda lb_, tl_: (lambda: norm_B(lb_, 1, tl_, 7)))(lb, tl(segs[-1])))

            def even_mixer(l, segs):
                i2 = l // 2
                Sd.barrier()
                A.reset()
                NPB = 10
                KT = A.bf16(NT)
                V = A.bf16(18, 128)
                Qs = A.bf16(4, 512)
                Us = A.bf16(6, 512)
                Pb = [A.bf16(512) for _ in range(NPB)]
                Db = A.bf16(4, 512)
                Ys = A.bf16(8, 512)
                BND = A.bf16(20, 128)
                MSK = A.bf16(2, 128)
                PW = A.bf16(4, 128)
                rc = [A.f32(512) for _ in range(1)]
                rs = [A.f32(512) for _ in range(1)]
                t1 = A.f32(512)
                UB = [A.bf16(512) for _ in range(2)]
                PRM = A.bf16(128)
                SINK = A.f32(4)
                dn = [A.f32(512) for _ in range(1)]
                rot = {"i": 0}
                cnt = {"rope": 0, "p": 0, "dn": 0, "pv": 0}
                Sd.fence("pool")
                Sd.fence("sp")
                Sd.dma("pool", ch_constp, BND[:, :, :], bandsd.rearrange("p (a b) -> p a b", a=20), writes=[("BND",)])
                Sd.dma("pool", ch_constp, MSK[:, :, :], masksd.rearrange("p (a b) -> p a b", a=2), writes=[("MSK",)])
                Sd.dma("pool", ch_constp, PW[:, :, :], pool_w[i2].rearrange("g c d -> c g d"), writes=[("PW",)])
                Sd.dma("pool", ch_constp, PRM[:, :], permd, writes=[("PRM",)])
                setw([("BND",), ("MSK",), ("PW",), ("PRM",)], ch_constp)
                ACT(SINK[:, 0:4], vec("sinkc", i2 * 4, i2 * 4 + 4), AF.Exp, VK, [("SINK",)])

                def load_rope(t0, n):
                    b = cnt["rope"] % len(rc)
                    cnt["rope"] += 1
                    Sd.dma("sp", ch_rope[b], rc[b][:, 0:n], ropec[:, t0:t0 + n], writes=[("rc", b)])
                    Sd.dma("sp", ch_rope[b], rs[b][:, 0:n], ropes[:, t0:t0 + n], writes=[("rs", b)])
                    setw([("rc", b)], ch_rope[b])
                    return b

                ucnt = {"i": 0}

                def rope_a(pq, rb_, n):
                    ub = ucnt["i"] % 2
                    ucnt["i"] += 1
                    TT(t1[:, 0:n], PS[pq][:, 0:n], rc[rb_][:, 0:n], ALU.mult, [psk(pq), ("rc", rb_)], [("t1",)])
                    TT(UB[ub][:, 0:n], PS[pq][:, 0:n], rs[rb_][:, 0:n], ALU.mult, [psk(pq), ("rs", rb_)], [("UB", ub)])
                    return ub

                def rope_b(pq2, ub, n):
                    MM(PS[pq2][:, 0:n], PRM[:, :], UB[ub][:, 0:n], True, True, reads=[("PRM",), ("UB", ub)], writes=[psk(pq2)], last=True)

                def rope_c(pq2, n, dst, dkey):
                    TT(dst, t1[:, 0:n], PS[pq2][:, 0:n], ALU.add, [("t1",), psk(pq2)], [dkey])

                kvs, kvk = get_slab(wsl(w_in[i2], 0, 256), 8)
                for it, (t0, n, s, ti) in enumerate(x_tiles(True)):
                    rb_ = load_rope(t0, n)
                    for k in range(8):
                        MM(PS[0][:, 0:n], kvs[:, k, 0:128], H[:, k, t0:t0 + n], k == 0, k == 7,
                           reads=[kvk, ("H", ti)], writes=[psk(0)], last=(k == 7))
                    ub = rope_a(0, rb_, n)
                    nb = n // 128
                    pv = 2 + (it % 2)
                    for bl in range(nb):
                        for k in range(8):
                            MM(PS[pv][:, bl * 128:(bl + 1) * 128], H[:, k, t0 + bl * 128:t0 + (bl + 1) * 128], kvs[:, k, 128:256], k == 0, k == 7,
                               reads=[kvk, ("H", ti)], writes=[psk(pv)], last=(k == 7))
                    rope_b(1, ub, n)
                    rope_c(1, n, KT[:, t0:t0 + n], ("KT",))
                    b0 = t0 // 128
                    ACT(V[:, b0:b0 + nb, :], PS[pv][:, 0:nb * 128].rearrange("p (b d) -> p b d", b=nb), AF.Copy, [psk(pv)], [("V",)])
                    if it == 0:
                        flush_deferred()

                prev_tile = [None]
                for (t0, n, s, ti, lo, hi) in segs:
                    is_ctx = (s == 1)
                    nqb = n // 128
                    n0 = t0 // 128
                    rb_ = load_rope(t0, n)
                    qs_, qk_ = get_slab(wsl(w_in[i2], 256, 768), 8)

                    def qmm(c):
                        pq = (c % 2) * 2
                        for k in range(8):
                            MM(PS[pq][:, 0:n], qs_[:, k, c * 128:(c + 1) * 128], H[:, k, t0:t0 + n], k == 0, k == 7,
                               reads=[qk_, ("H", ti)], writes=[psk(pq)], last=(k == 7))

                    ubs = {}
                    qmm(0)
                    ubs[0] = rope_a(0, rb_, n)
                    for c in range(4):
                        pq = (c % 2) * 2
                        if c + 1 < 4:
                            qmm(c + 1)
                        rope_b(pq + 1, ubs[c], n)
                        rope_c(pq + 1, n, Qs[:, c, 0:n], ("Qs",))
                        if c + 1 < 4:
                            ubs[c + 1] = rope_a(((c + 1) % 2) * 2, rb_, n)
                    def att_scores(g, nl):
                        gs = slice(g * 64, (g + 1) * 64)
                        nblk = n0 + nl
                        kbs = [(16, None), (17, None)]
                        if not is_ctx:
                            for m, mk in ((nblk - 1, 0), (nblk, None), (nblk + 1, 1)):
                                if 0 <= m <= 15:
                                    kbs.append((m, mk))
                        ptiles = []
                        for (m, mk) in kbs:
                            sbk = cnt["p"] % 4
                            pi = cnt["p"] % NPB
                            cnt["p"] += 1
                            MM(PS[sbk][:, :], KT[gs, m * 128:(m + 1) * 128], Qs[gs, :, nl * 128:(nl + 1) * 128], True, True,
                               reads=[("KT",), ("Qs",)], writes=[psk(sbk)], last=True)
                            ACT(Pb[pi][:, :], PS[sbk][:, :], AF.Exp, [psk(sbk)], [("P", pi)], scale=0.125)
                            if mk is not None:
                                p3 = Pb[pi][:, :].rearrange("p (c q) -> p c q", c=4)
                                TT(p3, p3, MSK[:, mk, :].unsqueeze(1).broadcast_to([128, 4, 128]), ALU.mult, [("P", pi), ("MSK",)], [("P", pi)])
                            ptiles.append((m, pi))
                        return ptiles

                    def att_pv(g, nl, ptiles):
                        gs = slice(g * 64, (g + 1) * 64)
                        pn = 4 + 2 * (cnt["pv"] % 2)
                        pd = pn + 1
                        cnt["pv"] += 1
                        np_ = len(ptiles)
                        for ii, (m, pi) in enumerate(ptiles):
                            MM(PS[pn][:, :], V[:, m, :], Pb[pi][:, :], ii == 0, ii == np_ - 1,
                               reads=[("V",), ("P", pi)], writes=[psk(pn)], last=(ii == np_ - 1))
                        for ii, (m, pi) in enumerate(ptiles):
                            MM(PS[pd][:, :], ONE1[:, :], Pb[pi][:, :], ii == 0, ii == np_ - 1,
                               reads=[("c", "one1"), ("P", pi)], writes=[psk(pd)], last=(ii == np_ - 1))
                        dk = ("dn", g)
                        TT(dn[0][gs, :].rearrange("p (c q) -> p c q", c=4), PS[pd][gs, :].rearrange("p (c q) -> p c q", c=4),
                           SINK[gs, 0:4].unsqueeze(2).broadcast_to([64, 4, 128]), ALU.add, [psk(pd), ("SINK",)], [dk])
                        ACT(dn[0][gs, :], dn[0][gs, :], AF.Ln, [dk], [dk])
                        ACT(dn[0][gs, :], dn[0][gs, :], AF.Exp, [dk], [dk], scale=-1.0)
                        return (g, nl, pn)

                    def att_fin(g, nl, pn):
                        gs = slice(g * 64, (g + 1) * 64)
                        TT(Ys[gs, 4:8, nl * 128:(nl + 1) * 128], PS[pn][gs, :].rearrange("p (c q) -> p c q", c=4),
                           dn[0][gs, :].rearrange("p (c q) -> p c q", c=4), ALU.mult, [psk(pn), ("dn", g)], [("Ys",)])

                    prevg = None
                    prevf = None
                    for nl in range(nqb):
                        for g in range(2):
                            pt = att_scores(g, nl)
                            if prevg is not None:
                                f = att_pv(*prevg)
                                if prevf is not None:
                                    att_fin(*prevf)
                                prevf = f
                            prevg = (g, nl, pt)
                    f = att_pv(*prevg)
                    if prevf is not None:
                        att_fin(*prevf)
                    att_fin(*f)
                    us_, uk_ = get_slab(wsl(w_in[i2], 768, 1280), 8)
                    sb0 = lo // 128
                    sb1 = hi // 128 - 1
                    blks = [b for b in range(n0 - 1, n0 + nqb + 1) if sb0 <= b <= sb1]
                    for ui, b in enumerate(blks):
                        pu = ui % 2
                        tix = min(b // 4, 4)
                        for k in range(8):
                            MM(PS[pu][:, :], H[:, k, b * 128:(b + 1) * 128], us_[:, k, :], k == 0, k == 7,
                               reads=[uk_, ("H", tix)], writes=[psk(pu)], last=(k == 7))
                        ACT(Us[:, ui, :], PS[pu][:, :], AF.Copy, [psk(pu)], [("Us", ui)])
                    for g4 in range(4):
                        pdd = 2 + (g4 % 2)
                        for nl in range(nqb):
                            b = n0 + nl
                            srcs = []
                            for m in (b - 1, b, b + 1):
                                if m < sb0 or m > sb1:
                                    continue
                                if m == b:
                                    var_ = 0 if b == sb0 else (2 if b == sb1 else 1)
                                else:
                                    var_ = 3 if m == b - 1 else 4
                                srcs.append((blks.index(m), var_))
                            for ii, (ui, var_) in enumerate(srcs):
                                MM(PS[pdd][:, nl * 128:(nl + 1) * 128], Us[:, ui, g4 * 128:(g4 + 1) * 128], BND[:, g4 * 5 + var_, :],
                                   ii == 0, ii == len(srcs) - 1,
                                   reads=[("Us", ui), ("BND",)], writes=[psk(pdd)], last=(ii == len(srcs) - 1))
                        DVE("tensor_copy", (Db[:, g4, 0:n], PS[pdd][:, 0:n]), [psk(pdd)], [("Db", g4)])
                        pp = 6 + (g4 % 2)
                        MM(PS[pp][:, 0:n], PW[:, g4, :], Db[:, g4, 0:n], True, True,
                           reads=[("PW",), ("Db", g4)], writes=[psk(pp)], last=True)
                        ACT(Ys[:, g4, 0:n], PS[pp][:, 0:n], AF.Identity, [psk(pp)] + VK, [("Ys",)],
                            scale=vec("pscale", i2 * 4 + g4, i2 * 4 + g4 + 1))
                    if prev_tile[0] is not None:
                        norm_A(prev_tile[0])
                    slabs = [get_slab(wsl(w_out[i2], 0, 512), 8), get_slab(wsl(w_out[i2], 512, 1024), 8)]
                    proj_residual(slabs, Ys, ("Ys",), n, t0, s, ti, [0, 1, 2, 3], rot)
                    if prev_tile[0] is not None:
                        norm_B(l % 2, 1, prev_tile[0], 7)
                    prev_tile[0] = (t0, n, s, ti)
                norm_A(prev_tile[0])
                deferred.append((lambda lb_, tl_: (lambda: norm_B(lb_, 1, tl_, 7)))(l % 2, prev_tile[0]))
                if l == 0 and debug:
                    o_ = 8 * NT
                    dump(dbgb[:, o_:o_ + NT], KT[:, :], [("KT",)])
                    o_ += NT
                    dump(dbgb[:, o_:o_ + 18 * 128].rearrange("p (a b) -> p a b", a=18), V[:, :, :], [("V",)])
                    o_ += 18 * 128
                    dump(dbgb[:, o_:o_ + 8 * 512].rearrange("p (a b) -> p a b", a=8), Ys[:, :, :], [("Ys",)])
                    o_ += 8 * 512
                    dump(dbgb[:, o_:o_ + 4 * 512].rearrange("p (a b) -> p a b", a=4), Qs[:, :, :], [("Qs",)])
                    o_ += 4 * 512
                    dump(dbgb[:, o_:o_ + 6 * 512].rearrange("p (a b) -> p a b", a=6), Us[:, :, :], [("Us", i) for i in range(6)])

            ch_dbg = Sd.chan("dbg")

            def dump(dst, src, keys):
                if not debug:
                    return
                Sd.dma("sp", ch_dbg, dst, src, reads=keys)
                if not dry:
                    for nm in ("pe", "act", "dve"):
                        Sd.E[nm].q.append(("wait", ch_dbg.sem, ch_dbg.count))

            A.reset()
            mod_ring_alloc(False, 4)
            rst5 = [A.f32(512) for _ in range(5)]
            for it0, tile0 in enumerate(x_tiles(True)):
                norm_A(tile0)
                norm_B_stats(tile0, it0 % 2, rst5[it0])
            for j in range(4):
                mod_slab(0, j)
            mod_finish(0, 0)
            for it0, tile0 in enumerate(x_tiles(True)):
                norm_B_mod(0, 0, tile0, rst5[it0])
            for j in range(4, 12):
                mod_slab(0, j)
            mod_finish(0, 1)
            for l in range(n_layers):
                cur["lb"] = l % 2
                upd_ctx = l < 2
                segs = [(i * 512, 512, 0, i, 0, S) for i in range(4)]
                if upd_ctx:
                    segs.append((S, L, 1, 4, S, NT))
                if l == 0 and debug:
                    dump(dbgf[:, 16 * NT:16 * NT + 96], MODV2[0][:, :], [("modv", 0)])
                    dump(dbgf[:, 16 * NT + 96:16 * NT + 160], AB2[0][:, :], [("ab", 0)])
                    dump(dbgb[:, 0:8 * NT].rearrange("p (a b) -> p a b", a=8), H[:, :, :], [("H", i) for i in range(5)])
                if l % 2 == 0:
                    even_mixer(l, segs)
                else:
                    conv_mixer(l, segs)
                if l == 0 and debug:
                    dump(dbgf[:, 0:8 * NT].rearrange("p (a b) -> p a b", a=8), X[:, :, :], [("X", i) for i in range(5)])
                nl_ = (l + 1) if (l + 1) < n_layers else None
                is_final = (nl_ is None) and final_norm
                if nl_ is not None:
                    next_tiles = x_tiles(nl_ <= 2)
                elif is_final:
                    next_tiles = x_tiles(False)
                else:
                    next_tiles = []
                mlp_phase(l, x_tiles(upd_ctx), nl_, next_tiles, is_final)
                if l == 0 and debug:
                    dump(dbgf[:, 8 * NT:16 * NT].rearrange("p (a b) -> p a b", a=8), X[:, :, :], [("X", i) for i in range(5)])
            flush_deferred()
            if not final_norm:
                Sd.barrier()
                Sd.fence("sp")
                for (t0, n, s, ti) in x_tiles(False):
                    Sd.dma("sp", out_chan(("x", ti)), outT[:, t0:t0 + n].rearrange("(c p) t -> p c t", p=128), X[:, :, t0:t0 + n], reads=[("X", ti)])
            if not dry:
                for ch_ in ch_out.values():
                    Sd.E["sp"].q.append(("wait", ch_.sem, ch_.count))

        plan = []
        emit(Sched(nc, st, dry=True), plan)
        Sreal = Sched(nc, st, dry=False)
        emit(Sreal, plan)
        with nc.Block() as block:
            Sreal.flush(block)
    return nc


def _chunkT(v):
    v = np.asarray(v, np.float32).reshape(-1, 128)
    return np.ascontiguousarray(v.T)


def _consts():
    ident = np.eye(128, dtype=np.float32)
    bands = np.zeros((4, 5, 128, 128), np.float32)
    Sx = 384
    t = np.arange(Sx)
    for g, w in enumerate((2, 4, 8, 16)):
        lo = np.clip(t - w // 2, 0, Sx)
        hi = np.clip(t + w - w // 2, 0, Sx)
        cntv = (hi - lo).astype(np.float32)
        M = np.zeros((Sx, Sx), np.float32)
        for to in range(Sx):
            M[lo[to]:hi[to], to] = np.float32(1.0) / cntv[to]
            M[to, to] -= 1.0
        blk = lambda bi, bo: M[bi * 128:(bi + 1) * 128, bo * 128:(bo + 1) * 128]
        bands[g, 0] = blk(0, 0)
        bands[g, 1] = blk(1, 1)
        bands[g, 2] = blk(2, 2)
        bands[g, 3] = blk(0, 1)
        bands[g, 4] = blk(2, 1)
    bands = np.ascontiguousarray(bands.reshape(20, 128, 128).transpose(1, 0, 2).reshape(128, 20 * 128))
    j = np.arange(128)[:, None]
    q = np.arange(128)[None, :]
    m0 = (j >= q).astype(np.float32)
    m1 = (j <= q).astype(np.float32)
    masks = np.concatenate([m0, m1], axis=1)
    inv = (10000.0 ** (-np.arange(0, 32, 2, dtype=np.float32) / np.float32(32))).astype(np.float32)
    tt = np.arange(S)
    row = (tt // 64).astype(np.float32)
    col = (tt % 64).astype(np.float32)
    rc = np.ones((128, NT), np.float32)
    rs = np.zeros((128, NT), np.float32)
    for p in range(128):
        d = p % 64
        blk_ = d // 32
        r = d % 32
        ang = (row if blk_ == 0 else col) * inv[r % 16]
        rc[p, :S] = np.cos(ang.astype(np.float32))
        sn = np.sin(ang.astype(np.float32))
        rs[p, :S] = -sn if r < 16 else sn
    pidx = np.arange(128)
    dd = pidx % 64
    pp = (pidx // 64) * 64 + (dd // 32) * 32 + np.where(dd % 32 < 16, dd % 32 + 16, dd % 32 - 16)
    perm = np.zeros((128, 128), np.float32)
    perm[pp, pidx] = 1.0
    rs = np.ascontiguousarray(rs[pp, :])
    return ident, bands, np.ascontiguousarray(masks), rc, rs, perm


def _prep_shared(inp):
    f = lambda a: np.ascontiguousarray(np.asarray(a, np.float32))
    sh = {}
    sh["w_mod"] = f(inp["w_mod"])
    sh["bmod"] = np.concatenate([_chunkT(inp["b_mod"][l]) for l in range(DEPTH)], axis=1)
    sh["n1g"] = np.concatenate([_chunkT(inp["norm1_g"][l]) for l in range(DEPTH)], axis=1)
    sh["n2g"] = np.concatenate([_chunkT(inp["norm2_g"][l]) for l in range(DEPTH)], axis=1)
    d = np.arange(64)
    partner = (d // 32) * 32 + np.where(d % 32 < 16, d % 32 + 16, d % 32 - 16)
    kcols = np.concatenate([1024 + g * 64 + d for g in range(2)])
    kpcols = np.concatenate([1024 + g * 64 + partner for g in range(2)])
    vcols = np.arange(1152, 1280)
    qcols = np.concatenate([np.concatenate([512 + (g * 4 + c) * 64 + d for g in range(2)]) for c in range(4)])
    qpcols = np.concatenate([np.concatenate([512 + (g * 4 + c) * 64 + partner for g in range(2)]) for c in range(4)])
    ucols = np.arange(0, 512)
    cols = np.concatenate([kcols, vcols, qcols, ucols])
    sh["w_in"] = f(np.asarray(inp["mix_w_in"])[:, :, cols])
    sh["pool_w"] = f(inp["pool_w"])
    sh["pscale"] = np.concatenate([_chunkT(inp["pool_scale"][i]) for i in range(2)], axis=1)
    p = np.arange(128)
    sh["sinkc"] = np.concatenate([np.stack([np.asarray(inp["attn_sink"], np.float32)[i][(p // 64) * 4 + c] for c in range(4)], axis=1)
                                  for i in range(2)], axis=1).astype(np.float32)
    arows = np.concatenate([np.concatenate([512 + (g * 4 + c) * 64 + d for g in range(2)]) for c in range(4)])
    rows = np.concatenate([np.arange(512), arows])
    sh["w_out"] = f(np.asarray(inp["mix_w_out"])[:, rows, :])
    pcols = np.concatenate([np.concatenate([np.arange(2 * s_ * 128, 2 * s_ * 128 + 256), np.arange(1024 + 2 * s_ * 128, 1024 + 2 * s_ * 128 + 256)])
                            for s_ in range(4)])
    sh["pw1"] = f(np.asarray(inp["conv_w_pw1"])[:, :, pcols])
    dw = np.asarray(inp["conv_w_dw"], np.float32)
    sh["dwT"] = np.concatenate([np.ascontiguousarray(dw[i].reshape(31, 8, 128).transpose(2, 1, 0)).reshape(128, 248) for i in range(2)], axis=1)
    sh["bdw"] = np.concatenate([_chunkT(inp["conv_b_dw"][i]) for i in range(2)], axis=1)
    sh["lng"] = np.concatenate([_chunkT(inp["conv_ln_g"][i]) for i in range(2)], axis=1)
    sh["lnb"] = np.concatenate([_chunkT(inp["conv_ln_b"][i]) for i in range(2)], axis=1)
    sh["pw2"] = f(inp["conv_w_pw2"])
    sh["w1"] = f(inp["mlp_w1"])
    sh["w2"] = f(inp["mlp_w2"])
    sh["fing"] = _chunkT(inp["final_g"])
    ident, bands, masks, rc, rs, perm = _consts()
    sh["permc"] = perm
    sh["ident"] = ident
    sh["bands"] = bands
    sh["masks"] = masks
    sh["ropec"] = rc
    sh["ropes"] = rs
    return {k: np.ascontiguousarray(v, dtype=np.float32) for k, v in sh.items()}


def run(inputs, n_layers=DEPTH, final_norm=True, cores=8, trace=False, debug=False):
    x = np.asarray(inputs["x"], np.float32)
    ctx = np.asarray(inputs["ctx"], np.float32)
    c = np.asarray(inputs["c"], np.float32)
    c_ctx = np.asarray(inputs["c_ctx"], np.float32)
    sh = _prep_shared(inputs)
    in_maps = []
    for b in range(cores):
        m = dict(sh)
        m["xT"] = np.ascontiguousarray(np.concatenate([x[b].T, ctx[b].T], axis=1))
        cv = np.stack([_chunkT(c[b]), _chunkT(c_ctx)], axis=2).reshape(128, 16)
        m["cvec"] = np.ascontiguousarray(cv)
        in_maps.append(m)
    nc = build(n_layers, final_norm, debug)
    res = run_bass_kernel_spmd(nc, in_maps, core_ids=list(range(cores)), **({"trace": True} if trace else {}))
    out = np.stack([np.ascontiguousarray(r["outT"].T) for r in res.results], axis=0)
    return out.astype(np.float32), res


def kernel(**inputs):
    out, _ = run(inputs)
    return out
```
